# Optimizing a Trainium2 kernel written in Bass

```python
import jax, jax.numpy as jnp
from jax import lax
import numpy as np

D_MODEL = 2048
BATCH = 4
SEQ = 4096
DEPTH = 1
DEC_BATCH = 16
DEC_SEQ = 32
PAST_LEN = 4096

CHUNK = 64
N_META = 16
D_CONF = 1024
D_SCONV = 1024
CONF_WIDTH = 31
SCONV_WIDTH = 3
FFN_WIDTH = 3
D_FF = 5632
EPS = 1e-6
IN_SIZES = (D_CONF, D_CONF, D_SCONV, D_SCONV, D_SCONV, D_MODEL, D_MODEL)
IN_SPLITS = tuple(int(s) for s in np.cumsum(IN_SIZES)[:-1])
D_IN = sum(IN_SIZES)

kernel_name = "hybrid_streaming_conv_encoder_step"


def rmsnorm(x, g):
    xf = x.astype(jnp.float32)
    y = xf * lax.rsqrt(jnp.mean(xf * xf, axis=-1, keepdims=True) + EPS)
    return (y * g.astype(jnp.float32)).astype(x.dtype)


def layernorm(x, g, b):
    xf = x.astype(jnp.float32)
    mu = jnp.mean(xf, axis=-1, keepdims=True)
    xc = xf - mu
    var = jnp.mean(xc * xc, axis=-1, keepdims=True)
    y = xc * lax.rsqrt(var + EPS) * g.astype(jnp.float32) + b.astype(jnp.float32)
    return y.astype(x.dtype)


def causal_dwconv(hist, u, w):
    width = w.shape[0]
    xp = jnp.concatenate([hist.astype(u.dtype), u], axis=1)
    y = lax.conv_general_dilated(
        xp, w[:, None, :].astype(u.dtype), window_strides=(1,), padding="VALID",
        dimension_numbers=("NWC", "WIO", "NWC"), feature_group_count=u.shape[-1])
    return y, xp[:, xp.shape[1] - (width - 1):]


def trunk_layer(x, hist_a, hist_b, hist_f, g_pre_mix, w_in, conf_conv_w, conf_conv_b,
                conf_ln_g, conf_ln_b, w_conf_out, sconv_w, w_sconv_out, w_o, g_post_mix,
                g_pre_ffn, w_up, ffn_conv_w, w_down, g_post_ffn):
    h = rmsnorm(x, g_pre_mix)
    z = jnp.einsum("bld,de->ble", h, w_in)
    a_val, a_gate, s_b, s_c, s_x, gate_a, gate_b = jnp.split(z, IN_SPLITS, axis=-1)
    a = a_val * jax.nn.sigmoid(a_gate)
    a_conv, new_a = causal_dwconv(hist_a, a, conf_conv_w)
    a_conv = a_conv + conf_conv_b
    a_out = jnp.einsum("blc,cd->bld", jax.nn.silu(layernorm(a_conv, conf_ln_g, conf_ln_b)), w_conf_out)
    u = s_c * s_x
    u_conv, new_b = causal_dwconv(hist_b, u, sconv_w)
    b_out = jnp.einsum("blc,cd->bld", s_b * u_conv, w_sconv_out)
    merged = jax.nn.sigmoid(gate_a) * a_out + jax.nn.sigmoid(gate_b) * b_out
    mix = jnp.einsum("bld,de->ble", merged, w_o)
    x = x + rmsnorm(mix, g_post_mix)
    h2 = rmsnorm(x, g_pre_ffn)
    up = jnp.einsum("bld,df->blf", h2, w_up)
    up_conv, new_f = causal_dwconv(hist_f, up, ffn_conv_w)
    f_gate, f_val = jnp.split(up_conv, 2, axis=-1)
    f = jnp.einsum("blf,fd->bld", jax.nn.silu(f_gate) * f_val, w_down)
    x = x + rmsnorm(f, g_post_ffn)
    return x, new_a, new_b, new_f


def setup_inputs(seed: int = 0) -> dict:
    key = jax.random.key(seed)
    ks = jax.random.split(key, 24)
    nrm = lambda k, shape, scale: jax.random.normal(k, shape, jnp.float32) * scale
    gain = lambda k, n: 1.0 + 0.05 * jax.random.normal(k, (DEPTH, n), jnp.float32)
    return {
        "x_prompt": nrm(ks[0], (BATCH, SEQ, D_MODEL), 1.0),
        "x_sample": nrm(ks[1], (DEC_BATCH, DEC_SEQ, D_MODEL), 1.0),
        "state_conf_conv": nrm(ks[2], (DEPTH, DEC_BATCH, CONF_WIDTH - 1, D_CONF), 1.0),
        "state_sconv": nrm(ks[3], (DEPTH, DEC_BATCH, SCONV_WIDTH - 1, D_SCONV), 1.0),
        "state_ffn_conv": nrm(ks[4], (DEPTH, DEC_BATCH, FFN_WIDTH - 1, 2 * D_FF), 1.0),
        "meta_tokens": nrm(ks[5], (N_META, D_MODEL), 1.0),
        "g_pre_mix": gain(ks[6], D_MODEL),
        "w_in": nrm(ks[7], (DEPTH, D_MODEL, D_IN), D_MODEL ** -0.5),
        "conf_conv_w": nrm(ks[8], (DEPTH, CONF_WIDTH, D_CONF), CONF_WIDTH ** -0.5),
        "conf_conv_b": nrm(ks[9], (DEPTH, D_CONF), 0.02),
        "conf_ln_g": gain(ks[10], D_CONF),
        "conf_ln_b": nrm(ks[11], (DEPTH, D_CONF), 0.02),
        "w_conf_out": nrm(ks[12], (DEPTH, D_CONF, D_MODEL), D_CONF ** -0.5),
        "sconv_w": nrm(ks[13], (DEPTH, SCONV_WIDTH, D_SCONV), SCONV_WIDTH ** -0.5),
        "w_sconv_out": nrm(ks[14], (DEPTH, D_SCONV, D_MODEL), D_SCONV ** -0.5),
        "w_o": nrm(ks[15], (DEPTH, D_MODEL, D_MODEL), D_MODEL ** -0.5),
        "g_post_mix": gain(ks[16], D_MODEL),
        "g_pre_ffn": gain(ks[17], D_MODEL),
        "w_up": nrm(ks[18], (DEPTH, D_MODEL, 2 * D_FF), D_MODEL ** -0.5),
        "ffn_conv_w": nrm(ks[19], (DEPTH, FFN_WIDTH, 2 * D_FF), FFN_WIDTH ** -0.5),
        "w_down": nrm(ks[20], (DEPTH, D_FF, D_MODEL), D_FF ** -0.5),
        "g_post_ffn": gain(ks[21], D_MODEL),
    }


def reference(x_prompt, x_sample, state_conf_conv, state_sconv, state_ffn_conv, meta_tokens,
              g_pre_mix, w_in, conf_conv_w, conf_conv_b, conf_ln_g, conf_ln_b, w_conf_out,
              sconv_w, w_sconv_out, w_o, g_post_mix, g_pre_ffn, w_up, ffn_conv_w, w_down,
              g_post_ffn):
    dt = x_prompt.dtype
    meta = jnp.broadcast_to(meta_tokens.astype(dt)[None], (x_prompt.shape[0], N_META, D_MODEL))
    xp = jnp.concatenate([meta, x_prompt], axis=1)
    xs = x_sample
    pa, pb, pf, sa, sb, sf = [], [], [], [], [], []
    for l in range(DEPTH):
        weights = (g_pre_mix[l], w_in[l], conf_conv_w[l], conf_conv_b[l], conf_ln_g[l],
                   conf_ln_b[l], w_conf_out[l], sconv_w[l], w_sconv_out[l], w_o[l],
                   g_post_mix[l], g_pre_ffn[l], w_up[l], ffn_conv_w[l], w_down[l], g_post_ffn[l])
        bp = xp.shape[0]
        xp, na, nb, nf = trunk_layer(
            xp, jnp.zeros((bp, CONF_WIDTH - 1, D_CONF), dt), jnp.zeros((bp, SCONV_WIDTH - 1, D_SCONV), dt),
            jnp.zeros((bp, FFN_WIDTH - 1, 2 * D_FF), dt), *weights)
        pa.append(na); pb.append(nb); pf.append(nf)
        xs, ma, mb, mf = trunk_layer(xs, state_conf_conv[l], state_sconv[l], state_ffn_conv[l], *weights)
        sa.append(ma); sb.append(mb); sf.append(mf)
    y_prompt = xp[:, N_META:]
    y_sample = xs
    return (y_prompt, y_sample, jnp.stack(pa), jnp.stack(pb), jnp.stack(pf),
            jnp.stack(sa), jnp.stack(sb), jnp.stack(sf))
```

```python
import numpy as np
import concourse.bass as bass
import concourse.mybir as mybir
from concourse.bass_utils import run_bass_kernel_spmd

F32 = mybir.dt.float32
BF16 = mybir.dt.bfloat16
ALU = mybir.AluOpType
AF = mybir.ActivationFunctionType

D = 2048
KC = 16
DC = 1024
CC = 8
DFF = 5632
FC = 44
UC = 88
DIN = 9216
NCORES = 8
NMETA = 16
SEQ = 4096
EXT = SEQ + NMETA
HALF = 2072
HALO = 32
SL = 32
GAP = 36
NT = HALF + 2 * (GAP + SL)
TW = 368
EPS = 1e-6
NSLOT = 5
SLOT_ELEMS = 4096

P_G1, P_G2, P_G3, P_G4 = 0, 16, 32, 48
P_CB, P_LG, P_LB = 64, 72, 80
P_CW = 88
P_SW = P_CW + 8 * 31
P_FW = P_SW + 8 * 3
NPAR = P_FW + 88 * 3

LASTP = HALF - 5 * TW
TILES = [[(0, 0, TW)] for _ in range(5)] + [[(0, 0, LASTP), (1, LASTP + GAP, SL), (2, LASTP + 2 * GAP + SL, SL)]]
assert LASTP + 2 * (GAP + SL) == TW and 6 * TW == NT and GAP >= 30


def make_blocks():
    blocks = []

    def add_block(w, r0, nk, c0, ncols):
        blocks.append((w, r0, nk, c0, ncols))
        return len(blocks) - 1

    seq_conf = []
    for i in range(4):
        bv = add_block("w_in", 0, KC, 256 * i, 256)
        bg = add_block("w_in", 0, KC, 1024 + 256 * i, 256)
        seq_conf.append((bv, bg))
    seq_sc = []
    for i in range(4):
        b_c = add_block("w_in", 0, KC, 3072 + 256 * i, 256)
        b_x = add_block("w_in", 0, KC, 4096 + 256 * i, 256)
        b_b = add_block("w_in", 0, KC, 2048 + 256 * i, 256)
        seq_sc.append((b_c, b_x, b_b))
    seq_gate = []
    for i in range(8):
        ba = add_block("w_in", 0, KC, 5120 + 256 * i, 256)
        bb = add_block("w_in", 0, KC, 7168 + 256 * i, 256)
        seq_gate.append((ba, bb))
    seq_out = []
    for q in range(4):
        b1 = add_block("w_conf_out", 0, CC, 512 * q, 512)
        b2 = add_block("w_sconv_out", 0, CC, 512 * q, 512)
        seq_out.append((b1, b2))
    seq_wo = [add_block("w_o", 0, KC, 256 * i, 256) for i in range(8)]
    seq_up = []
    for i in range(22):
        b1 = add_block("w_up", 0, KC, 256 * i, 256)
        b2 = add_block("w_up", 0, KC, DFF + 256 * i, 256)
        seq_up.append((b1, b2))
    seq_dn = []
    for m in range(16):
        b1 = add_block("w_down", 0, 22, 128 * m, 128)
        b2 = add_block("w_down", 22 * 128, 22, 128 * m, 128)
        seq_dn.append((b1, b2))
    return blocks, seq_conf, seq_sc, seq_gate, seq_out, seq_wo, seq_up, seq_dn


class Sem:
    def __init__(self, h, step):
        self.h = h
        self.step = step
        self.val = 0


class Prog:
    STREAMS = ("pe", "act", "dve", "pool", "sp")

    def __init__(self, csem):
        self.csem = csem
        self.q = {s: [] for s in self.STREAMS}
        self.waited = {s: {} for s in self.STREAMS}
        self.lastw = {}
        self.readers = {}

    def op(self, stream, reads, writes, emit, sem=None, ninc=1):
        sem = sem or self.csem[stream]
        need = {}
        for r in reads:
            lw = self.lastw.get(r)
            if lw and need.get(lw[0], 0) < lw[1]:
                need[lw[0]] = lw[1]
        for w in writes:
            lw = self.lastw.get(w)
            if lw and need.get(lw[0], 0) < lw[1]:
                need[lw[0]] = lw[1]
            for s, v in self.readers.get(w, {}).items():
                if need.get(s, 0) < v:
                    need[s] = v
        wl = []
        for s, v in need.items():
            if stream == "pe" and s is self.csem["pe"]:
                continue
            if self.waited[stream].get(s, 0) >= v:
                continue
            self.waited[stream][s] = v
            wl.append((s, v))
        sem.val += sem.step * ninc
        myval = sem.val
        self.q[stream].append((wl, emit, sem, ninc))
        for r in reads:
            d = self.readers.setdefault(r, {})
            if d.get(sem, 0) < myval:
                d[sem] = myval
        for w in writes:
            self.lastw[w] = (sem, myval)
            self.readers[w] = {}

    def run(self, stream, e):
        for wl, emit, sem, ninc in self.q[stream]:
            for s, v in wl:
                e.wait_ge(s.h, v)
            res = emit(e)
            ins = res if isinstance(res, list) else [res]
            assert len(ins) == ninc
            for i in ins:
                i.then_inc(sem.h, sem.step)


def build_nc():
    nc = bass.Bass("TRN2", target_bir_lowering=False)
    dt_in = lambda name, shape: nc.dram_tensor(name, shape, F32, kind="ExternalInput").ap()
    xT = dt_in("xT", [D, NT])
    params = dt_in("params", [128, NPAR])
    stA = dt_in("stA", [2, 128, CC * 30])
    stU = dt_in("stU", [2, 128, CC * 2])
    stF = dt_in("stF", [2, 128, UC * 2])
    yT = nc.dram_tensor("yT", [D, NT], F32, kind="ExternalOutput").ap()
    oA = nc.dram_tensor("oA", [3, 128, CC * 30], F32, kind="ExternalOutput").ap()
    oU = nc.dram_tensor("oU", [3, 128, CC * 2], F32, kind="ExternalOutput").ap()
    oF = nc.dram_tensor("oF", [3, 128, UC * 2], F32, kind="ExternalOutput").ap()
    NBLK = 128
    wblk = dt_in("wblk", [NBLK, 128, SLOT_ELEMS])
    scr = nc.dram_tensor("wscr", [NBLK, 128, SLOT_ELEMS], BF16, kind="Internal").ap()

    off = [0]
    plan = {}

    def alloc(name, nbytes):
        nb = (nbytes + 31) // 32 * 32
        plan[name] = off[0]
        off[0] += nb
        return plan[name]

    WA = TW + 30
    WF = TW + 2
    a_X = alloc("X", KC * TW * 4)
    a_H = alloc("H", KC * TW * 2)
    a_M = alloc("M", KC * TW * 4)
    a_sq = alloc("sq", 4 * TW * 2)
    a_st = alloc("st", 4 * TW * 4)
    a_par = alloc("par", NPAR * 4)
    a_hA = alloc("hA", 3 * CC * 30 * 4)
    a_hU = alloc("hU", 3 * CC * 2 * 4)
    a_hF = alloc("hF", 3 * UC * 2 * 4)
    a_one = alloc("ones", 2 * 128 * 2)
    a_eps = alloc("eps", 32)
    a_dum = alloc("dummy", 32)
    a_acc1 = alloc("acc1", 2 * (WA - 30) * 4)
    a_ws = alloc("ws", NSLOT * SLOT_ELEMS * 2)
    a_reg = off[0]
    a_abuf = alloc("abuf", CC * WA * 4)
    a_aconv = alloc("aconv", CC * (WA - 30) * 4)
    a_sa = alloc("sa", CC * TW * 2)
    a_sg = alloc("sg", 2 * TW * 4)
    a_sct = alloc("sct", 2 * TW * 4)
    a_ubuf = alloc("ubuf", 2 * WF * 4)
    a_uc = alloc("uc", 2 * TW * 4)
    a_bin = alloc("bin", CC * TW * 2)
    a_gat = alloc("gat", 32 * TW * 2)
    a_t12 = alloc("t12", 4 * TW * 4)
    a_mg = alloc("mg", KC * TW * 2)
    endB = off[0]
    off[0] = a_reg
    a_ub = alloc("ub", 4 * WF * 4)
    a_cv = alloc("cv", 4 * TW * 4)
    a_sil = alloc("sil", 2 * TW * 4)
    a_g = alloc("g", FC * TW * 2)
    endC = off[0]
    total = max(endB, endC)
    assert total <= 212000, total

    arena = nc.alloc_sbuf_tensor("arena", [128, total // 4 + 8], F32)
    arena_bf = arena.bitcast(BF16)

    def vf(a, n):
        return arena[:, a // 4: a // 4 + n]

    def vb(a, n):
        return arena_bf[:, a // 2: a // 2 + n]

    X = vf(a_X, KC * TW).rearrange("p (a b) -> p a b", a=KC)
    H = vb(a_H, KC * TW).rearrange("p (a b) -> p a b", a=KC)
    M = vf(a_M, KC * TW).rearrange("p (a b) -> p a b", a=KC)
    SQ = vb(a_sq, 4 * TW).rearrange("p (a b) -> p a b", a=4)
    ST = vf(a_st, 4 * TW).rearrange("p (a b) -> p a b", a=4)
    PAR = vf(a_par, NPAR)
    HA = vf(a_hA, 3 * CC * 30).rearrange("p (s c t) -> p s c t", s=3, c=CC)
    HU = vf(a_hU, 3 * CC * 2).rearrange("p (s c t) -> p s c t", s=3, c=CC)
    HF = vf(a_hF, 3 * UC * 2).rearrange("p (s c t) -> p s c t", s=3, c=UC)
    ONES = vb(a_one, 256).rearrange("p (a b) -> p a b", a=2)
    EPSB = vf(a_eps, 1)
    DUMMY = vf(a_dum, 1)
    ACC1 = vf(a_acc1, 2 * (WA - 30)).rearrange("p (a b) -> p a b", a=2)
    WS = [vb(a_ws + s * SLOT_ELEMS * 2, SLOT_ELEMS) for s in range(NSLOT)]
    ABUF = vf(a_abuf, CC * WA).rearrange("p (a b) -> p a b", a=CC)
    ACONV = vf(a_aconv, CC * (WA - 30)).rearrange("p (a b) -> p a b", a=CC)
    SA = vb(a_sa, CC * TW).rearrange("p (a b) -> p a b", a=CC)
    SG = vf(a_sg, 2 * TW).rearrange("p (a b) -> p a b", a=2)
    SCT = vf(a_sct, 2 * TW).rearrange("p (a b) -> p a b", a=2)
    UBUF = vf(a_ubuf, 2 * WF).rearrange("p (a b) -> p a b", a=2)
    UCV = vf(a_uc, 2 * TW).rearrange("p (a b) -> p a b", a=2)
    BIN = vb(a_bin, CC * TW).rearrange("p (a b) -> p a b", a=CC)
    GAT = vb(a_gat, 32 * TW).rearrange("p (a b) -> p a b", a=32)
    XN = vf(a_gat, KC * TW).rearrange("p (a b) -> p a b", a=KC)
    T12 = vf(a_t12, 4 * TW).rearrange("p (a b) -> p a b", a=4)
    MG = vb(a_mg, KC * TW).rearrange("p (a b) -> p a b", a=KC)
    UB = vf(a_ub, 4 * WF).rearrange("p (a b) -> p a b", a=4)
    CV = vf(a_cv, 4 * TW).rearrange("p (a b) -> p a b", a=4)
    SIL = vf(a_sil, 2 * TW).rearrange("p (a b) -> p a b", a=2)
    G = vb(a_g, FC * TW).rearrange("p (a b) -> p a b", a=FC)

    def par(col):
        return PAR[:, col:col + 1]

    banks = [nc.alloc_psum_tensor("bank%d" % i, [128, 512], F32) for i in range(8)]

    sem_names = ["pe", "act", "dve", "pool", "x", "y", "const", "out"] + \
                ["wl%d" % i for i in range(NSLOT)] + ["wb%d" % i for i in range(NSLOT)]
    import contextlib
    with contextlib.ExitStack() as es:
        sh = {n: es.enter_context(nc.semaphore("s_" + n)) for n in sem_names}
        block = es.enter_context(nc.Block())
        csem = {k: Sem(sh[k], 1) for k in ("pe", "act", "dve", "pool")}
        s_x, s_y, s_const, s_out = (Sem(sh[k], 16) for k in ("x", "y", "const", "out"))
        s_wl = [Sem(sh["wl%d" % i], 16) for i in range(NSLOT)]
        s_wb = [Sem(sh["wb%d" % i], 16) for i in range(NSLOT)]
        P = Prog(csem)
        RG = "REGION"

        blocks, seq_conf, seq_sc, seq_gate, seq_out, seq_wo, seq_up, seq_dn = make_blocks()
        assert len(blocks) == NBLK
        NGB = NBLK * len(TILES)
        loaded = [0]

        def slot_view(gb):
            w, r0, nk, c0, ncols = blocks[gb % NBLK]
            return WS[gb % NSLOT][:, 0:nk * ncols].rearrange("p (a b) -> p a b", a=nk)

        def emit_load(gb):
            bi = gb % NBLK
            s = gb % NSLOT
            w, r0, nk, c0, ncols = blocks[bi]
            dst = slot_view(gb)
            sc = scr[bi][:, 0:nk * ncols].rearrange("p (a b) -> p a b", a=nk)
            tile_i = gb // NBLK
            late = (bi % 4 == 0)
            if tile_i == 0 or (tile_i == 1 and late):
                src = wblk[bi][:, 0:nk * ncols]
                dst2 = WS[s][:, 0:nk * ncols]
                P.op("pool", [], [("ws", s)], lambda e, dst2=dst2, src=src: e.dma_start(out=dst2, in_=src), sem=s_wl[s])
                if (tile_i == 1) == late:
                    P.op("sp", [("ws", s)], [("scr", bi)], lambda e, dst=dst, sc=sc: e.dma_start(out=sc, in_=dst), sem=s_wb[s])
            else:
                P.op("sp", [("scr", bi)], [("ws", s)], lambda e, dst=dst, sc=sc: e.dma_start(out=dst, in_=sc), sem=s_wl[s])

        def need_block(gb):
            while loaded[0] < min(gb + NSLOT, NGB):
                emit_load(loaded[0])
                loaded[0] += 1

        bank_ctr = [0]

        def next_bank():
            b = bank_ctr[0] % 7
            bank_ctr[0] += 1
            return b

        def pe_group(gbs, parts, rhs_list, rhs_res, T, extra_reads=(), oldest=None):
            need_block(min(gbs) if oldest is None else oldest)
            b = next_bank()
            out = banks[b][:, 0:T]
            n = len(parts)
            mm = []
            for i, (gb, kl, ml) in enumerate(parts):
                lhsT = slot_view(gb)[:, kl, ml * 128:(ml + 1) * 128]
                mm.append((lhsT, rhs_list[i]))

            def emit(e, mm=mm, out=out, n=n):
                last = None
                for i, (l, r) in enumerate(mm):
                    last = e.matmul(out, lhsT=l, rhs=r, start=(i == 0), stop=(i == n - 1))
                return last
            reads = [("ws", gb % NSLOT) for gb in gbs] + list(rhs_res) + list(extra_reads)
            P.op("pe", reads, [("bank", b)], emit)
            return b

        def emit_consts(e):
            ins = [e.dma_start(out=PAR, in_=params[:, :])]
            for i in range(2):
                ins.append(e.dma_start(out=HA[:, 1 + i].rearrange("p c t -> p (c t)"), in_=stA[i]))
                ins.append(e.dma_start(out=HU[:, 1 + i].rearrange("p c t -> p (c t)"), in_=stU[i]))
                ins.append(e.dma_start(out=HF[:, 1 + i].rearrange("p c t -> p (c t)"), in_=stF[i]))
            return ins
        P.op("sp", [], ["PAR", "HA12", "HU12", "HF12"], emit_consts, sem=s_const, ninc=7)
        P.op("dve", [], ["ONES"], lambda e: e.memset(ONES[:, 0, :], 1.0 / D))
        P.op("dve", [], ["ONES"], lambda e: e.memset(ONES[:, 1, :], 1.0 / DC))
        P.op("dve", [], ["EPS"], lambda e: e.memset(EPSB, EPS))
        P.op("dve", [], ["HA0"], lambda e: e.memset(HA[:, 0].rearrange("p c t -> p (c t)"), 0.0))
        P.op("dve", [], ["HU0"], lambda e: e.memset(HU[:, 0].rearrange("p c t -> p (c t)"), 0.0))
        P.op("dve", [], ["HF0"], lambda e: e.memset(HF[:, 0].rearrange("p c t -> p (c t)"), 0.0))
        P.op("dve", [], [("abuf", c) for c in range(CC)] + [RG], lambda e: e.memset(vf(a_abuf, CC * WA), 0.0))
        P.op("dve", [], [("ubuf", 0), ("ubuf", 1), RG], lambda e: e.memset(vf(a_ubuf, 2 * WF), 0.0))

        def hres(kind, seq):
            return ("h" + kind, seq)

        def hist_reads(kind, seq):
            return [("H%s0" % kind) if seq == 0 else ("H%s12" % kind), hres(kind, seq)]

        def norm_stats(T, produce_sq, oneidx, nk, b=7):
            out = banks[b][:, 0:T]
            for k in range(nk):
                slot = k % 4
                produce_sq(k, slot)
                P.op("pe", [("sq", slot), "ONES"], [("bank", b)],
                     lambda e, k=k, slot=slot, out=out: e.matmul(out, lhsT=ONES[:, oneidx, :], rhs=SQ[:, slot, 0:T],
                                                                start=(k == 0), stop=(k == nk - 1)))
            return b

        def rstd_from_bank(b, T, dst_idx):
            P.op("act", ["EPS"], [("bank", b), ("st", dst_idx)],
                 lambda e: e.activation(out=ST[:, dst_idx, 0:T], in_=banks[b][:, 0:T], func=AF.Sqrt, bias=EPSB, scale=1.0))
            P.op("dve", [], [("st", dst_idx)],
                 lambda e: e.reciprocal(out=ST[:, dst_idx, 0:T], in_=ST[:, dst_idx, 0:T]))

        def emit_tile(ti, pts, col0):
            T = TW
            gb0 = ti * NBLK
            segs = [(0, T)]
            co, ao, uo = [0], [0], [0]
            Wc = T
            Wu = T

            P.op("dve", [], [RG], lambda e: e.memset(DUMMY, 0.0))

            def sq_x(k, slot, T=T):
                P.op("act", [("X", k)], [("sq", slot)],
                     lambda e: e.activation(out=SQ[:, slot, 0:T], in_=X[:, k, 0:T], func=AF.Square))
            if ti == 0:
                xsrc = xT[:, col0:col0 + T].rearrange("(kc p) t -> p kc t", p=128)
                P.op("sp", [], [("X", k) for k in range(KC)],
                     lambda e, xsrc=xsrc, T=T: e.dma_start(out=X[:, :, 0:T], in_=xsrc), sem=s_x)
                b = norm_stats(T, sq_x, 0, KC)
                rstd_from_bank(b, T, 0)
                for k in range(KC):
                    P.op("dve", [("X", k), ("st", 0), "PAR"], [("H", k)],
                         lambda e, k=k, T=T: e.scalar_tensor_tensor(out=H[:, k, 0:T], in0=X[:, k, 0:T], scalar=par(P_G1 + k),
                                                                   in1=ST[:, 0, 0:T], op0=ALU.mult, op1=ALU.mult))
            else:
                for k in range(KC):
                    if True:
                        P.op("act", [("gat", 2 * k), ("gat", 2 * k + 1)], [("X", k)],
                             lambda e, k=k, T=T: e.activation(out=X[:, k, 0:T], in_=XN[:, k, 0:T], func=AF.Copy))
                    else:
                        P.op("pool", [("gat", 2 * k), ("gat", 2 * k + 1)], [("X", k)],
                             lambda e, k=k, T=T: e.tensor_copy(out=X[:, k, 0:T], in_=XN[:, k, 0:T]))
            Hres = [("H", k) for k in range(KC)]
            Hrhs = [H[:, k, 0:T] for k in range(KC)]

            bgq = []

            def drain_bg(n):
                for _ in range(min(n, len(bgq))):
                    bgq.pop(0)()

            def win_group(blk, ml, oldest):
                gb = gb0 + blk
                return pe_group([gb], [(gb, k, ml) for k in range(KC)], Hrhs, Hres, T, oldest=gb0 + oldest)

            for i in range(4):
                bv, bg = seq_conf[i]
                for cc in range(2):
                    c = 2 * i + cc
                    b_val = win_group(bv, cc, bv)
                    b_gate = win_group(bg, cc, bv)
                    sl = c % 2
                    P.op("act", [RG], [("bank", b_gate), ("sg", sl)],
                         lambda e, b_gate=b_gate, sl=sl, T=T: e.activation(out=SG[:, sl, 0:T], in_=banks[b_gate][:, 0:T], func=AF.Sigmoid))
                    for si, (seq, L) in enumerate(segs):
                        P.op("dve", [("sg", sl), RG], [("bank", b_val), ("abuf", c)],
                             lambda e, b_val=b_val, sl=sl, c=c, si=si, L=L: e.tensor_tensor(
                                 out=ABUF[:, c, ao[si] + 30: ao[si] + 30 + L], in0=banks[b_val][:, co[si]:co[si] + L],
                                 in1=SG[:, sl, co[si]:co[si] + L], op=ALU.mult))

                    def bg_conv(c=c):
                        for (seq, s0, L) in pts:
                            P.op("act", hist_reads("A", seq) + [RG], [("abuf", c)],
                                 lambda e, s0=s0, seq=seq: e.activation(out=ABUF[:, c, s0:s0 + 30], in_=HA[:, seq, c, :], func=AF.Copy))
                        P.op("act", [("abuf", c), "PAR", RG], [("aconv", c)],
                             lambda e: e.activation(out=ACONV[:, c, 0:Wc], in_=ABUF[:, c, 0:Wc], func=AF.Identity,
                                                    bias=par(P_CB + c), scale=par(P_CW + c * 31)))
                        P.op("act", [("abuf", c), "PAR", RG], [("acc1", c % 2)],
                             lambda e: e.activation(out=ACC1[:, c % 2, 0:Wc], in_=ABUF[:, c, 1:1 + Wc], func=AF.Identity,
                                                    scale=par(P_CW + c * 31 + 1)))
                    bgq.append(bg_conv)
                    for k in range(2, 31):
                        def bg_tap(c=c, k=k):
                            if k % 2 == 0:
                                acc, accres = ACONV[:, c, 0:Wc], ("aconv", c)
                            else:
                                acc, accres = ACC1[:, c % 2, 0:Wc], ("acc1", c % 2)
                            P.op("dve", [("abuf", c), "PAR", RG], [accres],
                                 lambda e: e.scalar_tensor_tensor(out=acc, in0=ABUF[:, c, k:k + Wc],
                                                                  scalar=par(P_CW + c * 31 + k), in1=acc,
                                                                  op0=ALU.mult, op1=ALU.add))
                        bgq.append(bg_tap)

                    def bg_hout(c=c):
                        P.op("dve", [("acc1", c % 2), RG], [("aconv", c)],
                             lambda e: e.tensor_tensor(out=ACONV[:, c, 0:Wc], in0=ACONV[:, c, 0:Wc], in1=ACC1[:, c % 2, 0:Wc], op=ALU.add))
                        for (seq, s0, L) in pts:
                            P.op("act", [("abuf", c), RG], [hres("A", seq)],
                                 lambda e, s0=s0, seq=seq, L=L: e.activation(out=HA[:, seq, c, :], in_=ABUF[:, c, s0 + L: s0 + L + 30], func=AF.Copy))
                    bgq.append(bg_hout)
                    drain_bg(6)

            for i in range(4):
                b_c, b_x, b_b = seq_sc[i]
                for cc in range(2):
                    c = 2 * i + cc
                    sl = c % 2
                    pc = win_group(b_c, cc, b_c)
                    px = win_group(b_x, cc, b_c)
                    pb = win_group(b_b, cc, b_c)
                    P.op("act", [RG], [("bank", pc), ("sct", sl)],
                         lambda e, pc=pc, sl=sl, T=T: e.activation(out=SCT[:, sl, 0:T], in_=banks[pc][:, 0:T], func=AF.Copy))
                    P.op("dve", [("sct", sl), RG], [("bank", px), ("ubuf", sl)],
                         lambda e, px=px, sl=sl: e.tensor_tensor(out=UBUF[:, sl, 2:2 + T], in0=banks[px][:, 0:T], in1=SCT[:, sl, 0:T], op=ALU.mult))
                    drain_bg(6)
                    for (seq, s0, L) in pts:
                        P.op("act", hist_reads("U", seq) + [RG], [("ubuf", sl)],
                             lambda e, sl=sl, s0=s0, seq=seq, c=c: e.activation(out=UBUF[:, sl, s0:s0 + 2], in_=HU[:, seq, c, :], func=AF.Copy))
                    P.op("act", [("ubuf", sl), "PAR", RG], [("uc", sl)],
                         lambda e, sl=sl, c=c: e.activation(out=UCV[:, sl, 0:Wu], in_=UBUF[:, sl, 0:Wu], func=AF.Identity, scale=par(P_SW + c * 3)))
                    for k in (1, 2):
                        P.op("dve", [("ubuf", sl), "PAR", RG], [("uc", sl)],
                             lambda e, sl=sl, c=c, k=k: e.scalar_tensor_tensor(out=UCV[:, sl, 0:Wu], in0=UBUF[:, sl, k:k + Wu],
                                                                              scalar=par(P_SW + c * 3 + k), in1=UCV[:, sl, 0:Wu],
                                                                              op0=ALU.mult, op1=ALU.add))
                    for (seq, s0, L) in pts:
                        P.op("act", [("ubuf", sl), RG], [hres("U", seq)],
                             lambda e, sl=sl, s0=s0, seq=seq, L=L, c=c: e.activation(out=HU[:, seq, c, :], in_=UBUF[:, sl, s0 + L: s0 + L + 2], func=AF.Copy))
                    P.op("dve", [("uc", sl), RG], [("bank", pb), ("bin", c)],
                         lambda e, pb=pb, sl=sl, c=c: e.tensor_tensor(out=BIN[:, c, 0:T], in0=banks[pb][:, 0:T], in1=UCV[:, sl, 0:T], op=ALU.mult))
                    drain_bg(6)

            def emit_ln():
                def seg_copy_act(func, c, slot):
                    for si, (seq, L) in enumerate(segs):
                        P.op("act", [("aconv", c), RG], [("sq", slot)],
                             lambda e, si=si, L=L: e.activation(out=SQ[:, slot, co[si]:co[si] + L], in_=ACONV[:, c, ao[si]:ao[si] + L], func=func))
                b_mean = norm_stats(T, lambda k, slot: seg_copy_act(AF.Copy, k, slot), 1, CC)
                b_ex2 = norm_stats(T, lambda k, slot: seg_copy_act(AF.Square, k, slot), 1, CC, b=next_bank())
                P.op("dve", [], [("bank", b_mean), ("st", 1)], lambda e, T=T, b_mean=b_mean: e.tensor_copy(out=ST[:, 1, 0:T], in_=banks[b_mean][:, 0:T]))
                P.op("dve", [("st", 1)], [("st", 2)], lambda e, T=T: e.tensor_tensor(out=ST[:, 2, 0:T], in0=ST[:, 1, 0:T], in1=ST[:, 1, 0:T], op=ALU.mult))
                P.op("dve", [], [("bank", b_ex2), ("st", 2)], lambda e, T=T, b_ex2=b_ex2: e.tensor_tensor(out=ST[:, 2, 0:T], in0=banks[b_ex2][:, 0:T], in1=ST[:, 2, 0:T], op=ALU.subtract))
                P.op("act", ["EPS"], [("st", 2)], lambda e, T=T: e.activation(out=ST[:, 2, 0:T], in_=ST[:, 2, 0:T], func=AF.Sqrt, bias=EPSB, scale=1.0))
                P.op("dve", [], [("st", 2)], lambda e, T=T: e.reciprocal(out=ST[:, 2, 0:T], in_=ST[:, 2, 0:T]))
                P.op("dve", [("st", 2)], [("st", 1)], lambda e, T=T: e.tensor_tensor(out=ST[:, 1, 0:T], in0=ST[:, 1, 0:T], in1=ST[:, 2, 0:T], op=ALU.mult))
                for c0_ in range(0, CC, 2):
                    for c in (c0_, c0_ + 1):
                        P.op("dve", [("st", 2), RG], [("aconv", c)],
                             lambda e, c=c: e.tensor_tensor(out=ACONV[:, c, 0:T], in0=ACONV[:, c, 0:T], in1=ST[:, 2, 0:T], op=ALU.mult))
                    for c in (c0_, c0_ + 1):
                        P.op("dve", [("st", 1), RG], [("aconv", c)],
                             lambda e, c=c: e.tensor_tensor(out=ACONV[:, c, 0:T], in0=ACONV[:, c, 0:T], in1=ST[:, 1, 0:T], op=ALU.subtract))
                    for c in (c0_, c0_ + 1):
                        P.op("act", [("aconv", c), "PAR", RG], [("sa", c)],
                             lambda e, c=c: e.activation(out=SA[:, c, 0:T], in_=ACONV[:, c, 0:T],
                                                         func=AF.Silu, bias=par(P_LB + c), scale=par(P_LG + c)))

            for i in range(8):
                ba, bb = seq_gate[i]
                if i == 5:
                    drain_bg(len(bgq))
                    emit_ln()
                for cc in range(2):
                    m = 2 * i + cc
                    for which, blk in ((0, ba), (1, bb)):
                        pg = win_group(blk, cc, ba)
                        gi = which * 16 + m
                        P.op("act", [RG], [("bank", pg), ("gat", gi)],
                             lambda e, pg=pg, gi=gi, T=T: e.activation(out=GAT[:, gi, 0:T], in_=banks[pg][:, 0:T], func=AF.Sigmoid))
                        drain_bg(8)

            SAres = [("sa", c) for c in range(CC)]
            SArhs = [SA[:, c, 0:T] for c in range(CC)]
            BIres = [("bin", c) for c in range(CC)]
            BIrhs = [BIN[:, c, 0:T] for c in range(CC)]
            for q in range(4):
                b1, b2 = seq_out[q]
                for ml in range(4):
                    m = 4 * q + ml
                    sl = m % 2
                    pa = pe_group([gb0 + b1], [(gb0 + b1, k, ml) for k in range(CC)], SArhs, SAres, T, [RG], oldest=gb0 + b1)
                    pb = pe_group([gb0 + b2], [(gb0 + b2, k, ml) for k in range(CC)], BIrhs, BIres, T, [RG], oldest=gb0 + b1)
                    P.op("dve", [("gat", m), RG], [("bank", pa), ("t12", sl)],
                         lambda e, pa=pa, sl=sl, m=m, T=T: e.tensor_tensor(out=T12[:, sl, 0:T], in0=banks[pa][:, 0:T], in1=GAT[:, m, 0:T], op=ALU.mult))
                    P.op("dve", [("gat", 16 + m), RG], [("bank", pb), ("t12", 2 + sl)],
                         lambda e, pb=pb, sl=sl, m=m, T=T: e.tensor_tensor(out=T12[:, 2 + sl, 0:T], in0=banks[pb][:, 0:T], in1=GAT[:, 16 + m, 0:T], op=ALU.mult))
                    P.op("dve", [("t12", sl), ("t12", 2 + sl), RG], [("mg", m)],
                         lambda e, sl=sl, m=m, T=T: e.tensor_tensor(out=MG[:, m, 0:T], in0=T12[:, sl, 0:T], in1=T12[:, 2 + sl, 0:T], op=ALU.add))

            MGres = [("mg", k) for k in range(KC)]
            MGrhs = [MG[:, k, 0:T] for k in range(KC)]
            mix_banks = []

            def evac_to_M(pm, m, gcol, slot, T=T):
                P.op("act", [], [("bank", pm), ("sq", slot)],
                     lambda e: e.activation(out=SQ[:, slot, 0:T], in_=banks[pm][:, 0:T], func=AF.Square))
                P.op("act", ["PAR"], [("bank", pm), ("M", m)],
                     lambda e: e.activation(out=M[:, m, 0:T], in_=banks[pm][:, 0:T], func=AF.Identity, scale=par(gcol + m)))

            bstat = 7

            def stat_mm(m):
                P.op("pe", [("sq", m % 4), "ONES"], [("bank", bstat)],
                     lambda e: e.matmul(banks[bstat][:, 0:T], lhsT=ONES[:, 0, :], rhs=SQ[:, m % 4, 0:T],
                                        start=(m == 0), stop=(m == KC - 1)))
            for i in range(8):
                blk = seq_wo[i]
                for ml in range(2):
                    m = 2 * i + ml
                    pm = pe_group([gb0 + blk], [(gb0 + blk, k, ml) for k in range(KC)], MGrhs, MGres, T, [RG])
                    evac_to_M(pm, m, P_G2, m % 4)
                    if m >= 2:
                        stat_mm(m - 2)
            stat_mm(KC - 2)
            stat_mm(KC - 1)
            rstd_from_bank(bstat, T, 0)
            def x1_mul(xeng, m):
                P.op(xeng, [("st", 0)], [("M", m)],
                     lambda e, m=m, T=T: e.tensor_tensor(out=M[:, m, 0:T], in0=M[:, m, 0:T], in1=ST[:, 0, 0:T], op=ALU.mult))

            def x1_add(xeng, m):
                P.op(xeng, [("M", m)], [("X", m)],
                     lambda e, m=m, T=T: e.tensor_tensor(out=X[:, m, 0:T], in0=X[:, m, 0:T], in1=M[:, m, 0:T], op=ALU.add))
            n_dve = KC if ti <= 1 else 11
            x1_mul("dve", 0)
            for m in range(n_dve):
                if m + 1 < n_dve:
                    x1_mul("dve", m + 1)
                x1_add("dve", m)
            for m in range(n_dve, KC):
                x1_mul("pool", m)
                x1_add("pool", m)

            b = norm_stats(T, sq_x, 0, KC)
            rstd_from_bank(b, T, 3)
            for k in range(KC):
                P.op("dve", [("X", k), ("st", 3), "PAR"], [("H", k)],
                     lambda e, k=k, T=T: e.scalar_tensor_tensor(out=H[:, k, 0:T], in0=X[:, k, 0:T], scalar=par(P_G3 + k),
                                                               in1=ST[:, 3, 0:T], op0=ALU.mult, op1=ALU.mult))

            P.op("dve", [], [RG], lambda e: e.memset(DUMMY, 0.0))

            has_next = ti + 1 < len(TILES)
            if has_next:
                Tn = TW
                coln = col0 + T
                xsrcn = xT[:, coln:coln + Tn].rearrange("(kc p) t -> p kc t", p=128)
                P.op("sp", [], [("gat", j) for j in range(32)],
                     lambda e: e.dma_start(out=XN[:, :, 0:Tn], in_=xsrcn), sem=s_x)

            def pro_sq(k):
                P.op("act", [("gat", 2 * k), ("gat", 2 * k + 1)], [("sq", k % 4)],
                     lambda e: e.activation(out=SQ[:, k % 4, 0:Tn], in_=XN[:, k, 0:Tn], func=AF.Square))

            def pro_mm(k):
                P.op("pe", [("sq", k % 4), "ONES"], [("bank", 7)],
                     lambda e: e.matmul(banks[7][:, 0:Tn], lhsT=ONES[:, 0, :], rhs=SQ[:, k % 4, 0:Tn],
                                        start=(k == 0), stop=(k == KC - 1)))

            ubc = [0]
            for i in range(22):
                if has_next and 10 <= i < 18:
                    pro_sq(2 * (i - 10))
                    pro_sq(2 * (i - 10) + 1)
                if has_next and 11 <= i < 19:
                    pro_mm(2 * (i - 11))
                    pro_mm(2 * (i - 11) + 1)
                if has_next and i == 19:
                    rstd_from_bank(7, Tn, 1)
                b1, b2 = seq_up[i]
                for cc in range(2):
                    j = 2 * i + cc
                    slots = []
                    for which, blk in ((0, b1), (1, b2)):
                        cidx = j + which * FC
                        pu = win_group(blk, cc, b1)
                        sl = ubc[0] % 4
                        ubc[0] += 1
                        slots.append(sl)
                        P.op("act", [RG], [("bank", pu), ("ub", sl)],
                             lambda e, sl=sl, pu=pu: e.activation(out=UB[:, sl, 2:2 + T], in_=banks[pu][:, 0:T], func=AF.Copy))
                        for (seq, s0, L) in pts:
                            P.op("act", hist_reads("F", seq) + [RG], [("ub", sl)],
                                 lambda e, sl=sl, s0=s0, seq=seq, cidx=cidx: e.activation(out=UB[:, sl, s0:s0 + 2], in_=HF[:, seq, cidx, :], func=AF.Copy))
                            P.op("act", [], [("bank", pu), hres("F", seq)],
                                 lambda e, s0=s0, seq=seq, L=L, pu=pu, cidx=cidx: e.activation(out=HF[:, seq, cidx, :], in_=banks[pu][:, s0 + L - 2:s0 + L], func=AF.Copy))
                        P.op("dve", [("ub", sl), "PAR", RG], [("cv", sl)],
                             lambda e, sl=sl, cidx=cidx: e.tensor_scalar(out=CV[:, sl, 0:Wu], in0=UB[:, sl, 2:2 + Wu], scalar1=par(P_FW + cidx * 3 + 2),
                                                                         scalar2=None, op0=ALU.mult))
                        for k in (1, 0):
                            P.op("dve", [("ub", sl), "PAR", RG], [("cv", sl)],
                                 lambda e, sl=sl, cidx=cidx, k=k: e.scalar_tensor_tensor(out=CV[:, sl, 0:Wu], in0=UB[:, sl, k:k + Wu],
                                                                                        scalar=par(P_FW + cidx * 3 + k), in1=CV[:, sl, 0:Wu],
                                                                                        op0=ALU.mult, op1=ALU.add))
                    sg_, sv_ = slots
                    ss = j % 2
                    for si, (seq, L) in enumerate(segs):
                        P.op("act", [("cv", sg_), RG], [("sil", ss)],
                             lambda e, sg_=sg_, ss=ss, si=si, L=L: e.activation(out=SIL[:, ss, co[si]:co[si] + L], in_=CV[:, sg_, uo[si]:uo[si] + L], func=AF.Silu))
                        P.op("dve", [("sil", ss), ("cv", sv_), RG], [("g", j)],
                             lambda e, sv_=sv_, ss=ss, si=si, L=L, j=j: e.tensor_tensor(out=G[:, j, co[si]:co[si] + L], in0=SIL[:, ss, co[si]:co[si] + L],
                                                                                       in1=CV[:, sv_, uo[si]:uo[si] + L], op=ALU.mult))

            if has_next:
                for k in range(KC):
                    P.op("dve", [("gat", 2 * k), ("gat", 2 * k + 1), ("st", 1), "PAR"], [("H", k)],
                         lambda e, k=k: e.scalar_tensor_tensor(out=H[:, k, 0:Tn], in0=XN[:, k, 0:Tn], scalar=par(P_G1 + k),
                                                              in1=ST[:, 1, 0:Tn], op0=ALU.mult, op1=ALU.mult))

            Gres = [("g", j) for j in range(FC)]
            bstat = 7
            for m in range(16):
                b1, b2 = seq_dn[m]
                parts = [(gb0 + b1, k, 0) for k in range(22)] + [(gb0 + b2, k, 0) for k in range(22)]
                pf = pe_group([gb0 + b1, gb0 + b2], parts, [G[:, j, 0:T] for j in range(FC)], Gres, T, [RG])
                evac_to_M(pf, m, P_G4, m % 4)
                if m >= 1:
                    stat_mm(m - 1)
            stat_mm(KC - 1)
            rstd_from_bank(bstat, T, 0)
            yeng = "pool" if (has_next and ti >= 1) else "dve"
            for m in range(KC):
                P.op(yeng, [("st", 0)], [("M", m)],
                     lambda e, m=m, T=T: e.tensor_tensor(out=M[:, m, 0:T], in0=M[:, m, 0:T], in1=ST[:, 0, 0:T], op=ALU.mult))
                P.op(yeng, [("X", m)], [("M", m)],
                     lambda e, m=m, T=T: e.tensor_tensor(out=M[:, m, 0:T], in0=M[:, m, 0:T], in1=X[:, m, 0:T], op=ALU.add))
            ydst = yT[:, col0:col0 + T].rearrange("(kc p) t -> p kc t", p=128)
            P.op("pool", [("M", m) for m in range(KC)], ["Y"],
                 lambda e, ydst=ydst, T=T: e.dma_start(out=ydst, in_=M[:, :, 0:T]), sem=s_y)

        col0 = 0
        for ti, pts in enumerate(TILES):
            emit_tile(ti, pts, col0)
            col0 += TW

        def emit_outs(e):
            ins = []
            for s in range(3):
                ins.append(e.dma_start(out=oA[s], in_=HA[:, s].rearrange("p c t -> p (c t)")))
                ins.append(e.dma_start(out=oU[s], in_=HU[:, s].rearrange("p c t -> p (c t)")))
                ins.append(e.dma_start(out=oF[s], in_=HF[:, s].rearrange("p c t -> p (c t)")))
            return ins
        allh = [hres(k, s) for k in "AUF" for s in range(3)] + ["HA0", "HU0", "HF0", "HA12", "HU12", "HF12"]
        P.op("sp", allh, ["OUT"], emit_outs, sem=s_out, ninc=9)
        final_sems = [s_y, s_out, s_x, s_const] + s_wl + s_wb

        @block.tensor
        def _(e):
            P.run("pe", e)

        @block.scalar
        def _(e):
            P.run("act", e)

        @block.vector
        def _(e):
            P.run("dve", e)

        @block.gpsimd
        def _(e):
            P.run("pool", e)

        @block.sync
        def _(e):
            P.run("sp", e)
            for s in final_sems:
                if s.val > 0:
                    e.wait_ge(s.h, s.val)
            for k in ("pe", "act", "dve"):
                e.wait_ge(csem[k].h, csem[k].val)
    return nc


_NC_CACHE = {}


def _get_nc():
    if "nc" not in _NC_CACHE:
        _NC_CACHE["nc"] = build_nc()
    return _NC_CACHE["nc"]


def _pack_params(g1, g2, g3, g4, cb, lg, lb, cw, sw, fw):
    p = np.zeros((128, NPAR), np.float32)
    p[:, P_G1:P_G1 + 16] = g1.reshape(16, 128).T
    p[:, P_G2:P_G2 + 16] = g2.reshape(16, 128).T
    p[:, P_G3:P_G3 + 16] = g3.reshape(16, 128).T
    p[:, P_G4:P_G4 + 16] = g4.reshape(16, 128).T
    p[:, P_CB:P_CB + 8] = cb.reshape(8, 128).T
    p[:, P_LG:P_LG + 8] = lg.reshape(8, 128).T
    p[:, P_LB:P_LB + 8] = lb.reshape(8, 128).T
    p[:, P_CW:P_CW + 248] = cw.T.reshape(8, 128, 31).transpose(1, 0, 2).reshape(128, 248)
    p[:, P_SW:P_SW + 24] = sw.T.reshape(8, 128, 3).transpose(1, 0, 2).reshape(128, 24)
    p[:, P_FW:P_FW + 264] = fw.T.reshape(88, 128, 3).transpose(1, 0, 2).reshape(128, 264)
    return p


def _state_to_dev(st):
    w1, C = st.shape
    return np.ascontiguousarray(st.T.reshape(C // 128, 128, w1).transpose(1, 0, 2).reshape(128, -1))


def _state_from_dev(a, w1):
    ncn = a.shape[1] // w1
    return np.ascontiguousarray(a.reshape(128, ncn, w1).transpose(2, 1, 0).reshape(w1, ncn * 128))


def kernel(x_prompt, x_sample, state_conf_conv, state_sconv, state_ffn_conv, meta_tokens,
           g_pre_mix, w_in, conf_conv_w, conf_conv_b, conf_ln_g, conf_ln_b, w_conf_out,
           sconv_w, w_sconv_out, w_o, g_post_mix, g_pre_ffn, w_up, ffn_conv_w, w_down, g_post_ffn):
    f = lambda a: np.asarray(a, dtype=np.float32)
    x_prompt, x_sample, meta_tokens = f(x_prompt), f(x_sample), f(meta_tokens)
    sA, sU, sF = f(state_conf_conv)[0], f(state_sconv)[0], f(state_ffn_conv)[0]
    params = _pack_params(f(g_pre_mix)[0], f(g_post_mix)[0], f(g_pre_ffn)[0], f(g_post_ffn)[0], f(conf_conv_b)[0],
                          f(conf_ln_g)[0], f(conf_ln_b)[0], f(conf_conv_w)[0], f(sconv_w)[0], f(ffn_conv_w)[0])
    wsrc = {"w_in": f(w_in)[0], "w_conf_out": f(w_conf_out)[0], "w_sconv_out": f(w_sconv_out)[0],
            "w_o": f(w_o)[0], "w_up": f(w_up)[0], "w_down": f(w_down)[0]}
    blocks = make_blocks()[0]
    wblk = np.zeros((len(blocks), 128, SLOT_ELEMS), np.float32)
    for bi, (wn, r0, nk, c0, ncols) in enumerate(blocks):
        blk = wsrc[wn][r0:r0 + nk * 128, c0:c0 + ncols]
        wblk[bi, :, :nk * ncols] = blk.reshape(nk, 128, ncols).transpose(1, 0, 2).reshape(128, nk * ncols)
    shared = {"params": params, "wblk": wblk}
    in_maps = []
    for c in range(NCORES):
        b, half = c // 2, c % 2
        ext = np.concatenate([meta_tokens, x_prompt[b]], axis=0)
        part = ext[0:HALF] if half == 0 else ext[EXT - HALF:EXT]
        zg = np.zeros((GAP, D), np.float32)
        rows = np.concatenate([part, zg, x_sample[2 * c], zg, x_sample[2 * c + 1]], axis=0)
        m = dict(shared)
        m["xT"] = np.ascontiguousarray(rows.T)
        m["stA"] = np.stack([_state_to_dev(sA[2 * c + i]) for i in range(2)])
        m["stU"] = np.stack([_state_to_dev(sU[2 * c + i]) for i in range(2)])
        m["stF"] = np.stack([_state_to_dev(sF[2 * c + i]) for i in range(2)])
        in_maps.append(m)
    nc = _get_nc()
    res = run_bass_kernel_spmd(nc, in_maps, core_ids=list(range(NCORES)))
    B = x_prompt.shape[0]
    y_prompt = np.zeros((B, SEQ, D), np.float32)
    y_sample = np.zeros((2 * NCORES, SL, D), np.float32)
    ncp = np.zeros((1, B, 30, DC), np.float32)
    nsp = np.zeros((1, B, 2, DC), np.float32)
    nfp = np.zeros((1, B, 2, 2 * DFF), np.float32)
    ncs = np.zeros((1, 2 * NCORES, 30, DC), np.float32)
    nss = np.zeros((1, 2 * NCORES, 2, DC), np.float32)
    nfs = np.zeros((1, 2 * NCORES, 2, 2 * DFF), np.float32)
    first = HALF - NMETA
    for c in range(NCORES):
        r = res.results[c]
        b, half = c // 2, c % 2
        y = np.asarray(r["yT"]).T
        if half == 0:
            y_prompt[b, 0:first] = y[NMETA:HALF]
        else:
            y_prompt[b, first:SEQ] = y[HALO:HALF]
            ncp[0, b] = _state_from_dev(np.asarray(r["oA"])[0], 30)
            nsp[0, b] = _state_from_dev(np.asarray(r["oU"])[0], 2)
            nfp[0, b] = _state_from_dev(np.asarray(r["oF"])[0], 2)
        for i in range(2):
            y_sample[2 * c + i] = y[HALF + GAP + (GAP + SL) * i: HALF + (GAP + SL) * (i + 1)]
            ncs[0, 2 * c + i] = _state_from_dev(np.asarray(r["oA"])[1 + i], 30)
            nss[0, 2 * c + i] = _state_from_dev(np.asarray(r["oU"])[1 + i], 2)
            nfs[0, 2 * c + i] = _state_from_dev(np.asarray(r["oF"])[1 + i], 2)
    return (y_prompt, y_sample, ncp, nsp, nfp, ncs, nss, nfs)
```

```python
import numpy as np
import concourse.bass as bass
import concourse.mybir as mybir
from concourse.bass_utils import run_bass_kernel_spmd

F32 = mybir.dt.float32
BF16 = mybir.dt.bfloat16
ALU = mybir.AluOpType
AF = mybir.ActivationFunctionType

D = 2048
KC = 16
DC = 1024
CC = 8
DFF = 5632
FC = 44
UC = 88
DIN = 9216
NCORES = 8
NMETA = 16
SEQ = 4096
EXT = SEQ + NMETA
HALF = 2072
HALO = 32
SL = 32
GAP = 36
NT = HALF + 2 * (GAP + SL)
TW = 368
EPS = 1e-6
NSLOT = 5
SLOT_ELEMS = 4096

P_G1, P_G2, P_G3, P_G4 = 0, 16, 32, 48
P_CB, P_LG, P_LB = 64, 72, 80
P_CW = 88
P_SW = P_CW + 8 * 31
P_FW = P_SW + 8 * 3
NPAR = P_FW + 88 * 3

LASTP = HALF - 5 * TW
TILES = [[(0, 0, TW)] for _ in range(5)] + [[(0, 0, LASTP), (1, LASTP + GAP, SL), (2, LASTP + 2 * GAP + SL, SL)]]
assert LASTP + 2 * (GAP + SL) == TW and 6 * TW == NT and GAP >= 30


def make_blocks():
    blocks = []

    def add_block(w, r0, nk, c0, ncols):
        blocks.append((w, r0, nk, c0, ncols))
        return len(blocks) - 1

    seq_conf = []
    for i in range(4):
        bv = add_block("w_in", 0, KC, 256 * i, 256)
        bg = add_block("w_in", 0, KC, 1024 + 256 * i, 256)
        seq_conf.append((bv, bg))
    seq_sc = []
    for i in range(4):
        b_c = add_block("w_in", 0, KC, 3072 + 256 * i, 256)
        b_x = add_block("w_in", 0, KC, 4096 + 256 * i, 256)
        b_b = add_block("w_in", 0, KC, 2048 + 256 * i, 256)
        seq_sc.append((b_c, b_x, b_b))
    seq_gate = []
    for i in range(8):
        ba = add_block("w_in", 0, KC, 5120 + 256 * i, 256)
        bb = add_block("w_in", 0, KC, 7168 + 256 * i, 256)
        seq_gate.append((ba, bb))
    seq_out = []
    for q in range(4):
        b1 = add_block("w_conf_out", 0, CC, 512 * q, 512)
        b2 = add_block("w_sconv_out", 0, CC, 512 * q, 512)
        seq_out.append((b1, b2))
    seq_wo = [add_block("w_o", 0, KC, 256 * i, 256) for i in range(8)]
    seq_up = []
    for i in range(22):
        b1 = add_block("w_up", 0, KC, 256 * i, 256)
        b2 = add_block("w_up", 0, KC, DFF + 256 * i, 256)
        seq_up.append((b1, b2))
    seq_dn = []
    for m in range(16):
        b1 = add_block("w_down", 0, 22, 128 * m, 128)
        b2 = add_block("w_down", 22 * 128, 22, 128 * m, 128)
        seq_dn.append((b1, b2))
    return blocks, seq_conf, seq_sc, seq_gate, seq_out, seq_wo, seq_up, seq_dn


class Sem:
    def __init__(self, h, step):
        self.h = h
        self.step = step
        self.val = 0


class Prog:
    STREAMS = ("pe", "act", "dve", "pool", "sp")

    def __init__(self, csem):
        self.csem = csem
        self.q = {s: [] for s in self.STREAMS}
        self.waited = {s: {} for s in self.STREAMS}
        self.lastw = {}
        self.readers = {}

    def op(self, stream, reads, writes, emit, sem=None, ninc=1):
        sem = sem or self.csem[stream]
        need = {}
        for r in reads:
            lw = self.lastw.get(r)
            if lw and need.get(lw[0], 0) < lw[1]:
                need[lw[0]] = lw[1]
        for w in writes:
            lw = self.lastw.get(w)
            if lw and need.get(lw[0], 0) < lw[1]:
                need[lw[0]] = lw[1]
            for s, v in self.readers.get(w, {}).items():
                if need.get(s, 0) < v:
                    need[s] = v
        wl = []
        for s, v in need.items():
            if stream == "pe" and s is self.csem["pe"]:
                continue
            if self.waited[stream].get(s, 0) >= v:
                continue
            self.waited[stream][s] = v
            wl.append((s, v))
        sem.val += sem.step * ninc
        myval = sem.val
        self.q[stream].append((wl, emit, sem, ninc))
        for r in reads:
            d = self.readers.setdefault(r, {})
            if d.get(sem, 0) < myval:
                d[sem] = myval
        for w in writes:
            self.lastw[w] = (sem, myval)
            self.readers[w] = {}

    def run(self, stream, e):
        for wl, emit, sem, ninc in self.q[stream]:
            for s, v in wl:
                e.wait_ge(s.h, v)
            res = emit(e)
            ins = res if isinstance(res, list) else [res]
            assert len(ins) == ninc
            for i in ins:
                i.then_inc(sem.h, sem.step)


def build_nc():
    nc = bass.Bass("TRN2", target_bir_lowering=False)
    dt_in = lambda name, shape: nc.dram_tensor(name, shape, F32, kind="ExternalInput").ap()
    xT = dt_in("xT", [D, NT])
    params = dt_in("params", [128, NPAR])
    stA = dt_in("stA", [2, 128, CC * 30])
    stU = dt_in("stU", [2, 128, CC * 2])
    stF = dt_in("stF", [2, 128, UC * 2])
    yT = nc.dram_tensor("yT", [D, NT], F32, kind="ExternalOutput").ap()
    oA = nc.dram_tensor("oA", [3, 128, CC * 30], F32, kind="ExternalOutput").ap()
    oU = nc.dram_tensor("oU", [3, 128, CC * 2], F32, kind="ExternalOutput").ap()
    oF = nc.dram_tensor("oF", [3, 128, UC * 2], F32, kind="ExternalOutput").ap()
    NBLK = 128
    wblk = dt_in("wblk", [NBLK, 128, SLOT_ELEMS])
    scr = nc.dram_tensor("wscr", [NBLK, 128, SLOT_ELEMS], BF16, kind="Internal").ap()

    off = [0]
    plan = {}

    def alloc(name, nbytes):
        nb = (nbytes + 31) // 32 * 32
        plan[name] = off[0]
        off[0] += nb
        return plan[name]

    WA = TW + 30
    WF = TW + 2
    a_X = alloc("X", KC * TW * 4)
    a_H = alloc("H", KC * TW * 2)
    a_M = alloc("M", KC * TW * 4)
    a_sq = alloc("sq", 4 * TW * 2)
    a_st = alloc("st", 4 * TW * 4)
    a_par = alloc("par", NPAR * 4)
    a_hA = alloc("hA", 3 * CC * 30 * 4)
    a_hU = alloc("hU", 3 * CC * 2 * 4)
    a_hF = alloc("hF", 3 * UC * 2 * 4)
    a_one = alloc("ones", 2 * 128 * 2)
    a_eps = alloc("eps", 32)
    a_dum = alloc("dummy", 32)
    a_acc1 = alloc("acc1", 2 * (WA - 30) * 4)
    a_ws = alloc("ws", NSLOT * SLOT_ELEMS * 2)
    a_reg = off[0]
    a_abuf = alloc("abuf", CC * WA * 4)
    a_aconv = alloc("aconv", CC * (WA - 30) * 4)
    a_sa = alloc("sa", CC * TW * 2)
    a_sg = alloc("sg", 2 * TW * 4)
    a_sct = alloc("sct", 2 * TW * 4)
    a_ubuf = alloc("ubuf", 2 * WF * 4)
    a_uc = alloc("uc", 2 * TW * 4)
    a_bin = alloc("bin", CC * TW * 2)
    a_gat = alloc("gat", 32 * TW * 2)
    a_t12 = alloc("t12", 4 * TW * 4)
    a_mg = alloc("mg", KC * TW * 2)
    endB = off[0]
    off[0] = a_reg
    a_ub = alloc("ub", 4 * WF * 4)
    a_cv = alloc("cv", 4 * TW * 4)
    a_sil = alloc("sil", 2 * TW * 4)
    a_g = alloc("g", FC * TW * 2)
    endC = off[0]
    total = max(endB, endC)
    assert total <= 212000, total

    arena = nc.alloc_sbuf_tensor("arena", [128, total // 4 + 8], F32)
    arena_bf = arena.bitcast(BF16)

    def vf(a, n):
        return arena[:, a // 4: a // 4 + n]

    def vb(a, n):
        return arena_bf[:, a // 2: a // 2 + n]

    X = vf(a_X, KC * TW).rearrange("p (a b) -> p a b", a=KC)
    H = vb(a_H, KC * TW).rearrange("p (a b) -> p a b", a=KC)
    M = vf(a_M, KC * TW).rearrange("p (a b) -> p a b", a=KC)
    SQ = vb(a_sq, 4 * TW).rearrange("p (a b) -> p a b", a=4)
    ST = vf(a_st, 4 * TW).rearrange("p (a b) -> p a b", a=4)
    PAR = vf(a_par, NPAR)
    HA = vf(a_hA, 3 * CC * 30).rearrange("p (s c t) -> p s c t", s=3, c=CC)
    HU = vf(a_hU, 3 * CC * 2).rearrange("p (s c t) -> p s c t", s=3, c=CC)
    HF = vf(a_hF, 3 * UC * 2).rearrange("p (s c t) -> p s c t", s=3, c=UC)
    ONES = vb(a_one, 256).rearrange("p (a b) -> p a b", a=2)
    EPSB = vf(a_eps, 1)
    DUMMY = vf(a_dum, 1)
    ACC1 = vf(a_acc1, 2 * (WA - 30)).rearrange("p (a b) -> p a b", a=2)
    WS = [vb(a_ws + s * SLOT_ELEMS * 2, SLOT_ELEMS) for s in range(NSLOT)]
    ABUF = vf(a_abuf, CC * WA).rearrange("p (a b) -> p a b", a=CC)
    ACONV = vf(a_aconv, CC * (WA - 30)).rearrange("p (a b) -> p a b", a=CC)
    SA = vb(a_sa, CC * TW).rearrange("p (a b) -> p a b", a=CC)
    SG = vf(a_sg, 2 * TW).rearrange("p (a b) -> p a b", a=2)
    SCT = vf(a_sct, 2 * TW).rearrange("p (a b) -> p a b", a=2)
    UBUF = vf(a_ubuf, 2 * WF).rearrange("p (a b) -> p a b", a=2)
    UCV = vf(a_uc, 2 * TW).rearrange("p (a b) -> p a b", a=2)
    BIN = vb(a_bin, CC * TW).rearrange("p (a b) -> p a b", a=CC)
    GAT = vb(a_gat, 32 * TW).rearrange("p (a b) -> p a b", a=32)
    XN = vf(a_gat, KC * TW).rearrange("p (a b) -> p a b", a=KC)
    T12 = vf(a_t12, 4 * TW).rearrange("p (a b) -> p a b", a=4)
    MG = vb(a_mg, KC * TW).rearrange("p (a b) -> p a b", a=KC)
    UB = vf(a_ub, 4 * WF).rearrange("p (a b) -> p a b", a=4)
    CV = vf(a_cv, 4 * TW).rearrange("p (a b) -> p a b", a=4)
    SIL = vf(a_sil, 2 * TW).rearrange("p (a b) -> p a b", a=2)
    G = vb(a_g, FC * TW).rearrange("p (a b) -> p a b", a=FC)

    def par(col):
        return PAR[:, col:col + 1]

    banks = [nc.alloc_psum_tensor("bank%d" % i, [128, 512], F32) for i in range(8)]

    sem_names = ["pe", "act", "dve", "pool", "x", "y", "const", "out"] + \
                ["wl%d" % i for i in range(NSLOT)] + ["wb%d" % i for i in range(NSLOT)]
    import contextlib
    with contextlib.ExitStack() as es:
        sh = {n: es.enter_context(nc.semaphore("s_" + n)) for n in sem_names}
        block = es.enter_context(nc.Block())
        csem = {k: Sem(sh[k], 1) for k in ("pe", "act", "dve", "pool")}
        s_x, s_y, s_const, s_out = (Sem(sh[k], 16) for k in ("x", "y", "const", "out"))
        s_wl = [Sem(sh["wl%d" % i], 16) for i in range(NSLOT)]
        s_wb = [Sem(sh["wb%d" % i], 16) for i in range(NSLOT)]
        P = Prog(csem)
        RG = "REGION"

        blocks, seq_conf, seq_sc, seq_gate, seq_out, seq_wo, seq_up, seq_dn = make_blocks()
        assert len(blocks) == NBLK
        NGB = NBLK * len(TILES)
        loaded = [0]

        def slot_view(gb):
            w, r0, nk, c0, ncols = blocks[gb % NBLK]
            return WS[gb % NSLOT][:, 0:nk * ncols].rearrange("p (a b) -> p a b", a=nk)

        def emit_load(gb):
            bi = gb % NBLK
            s = gb % NSLOT
            w, r0, nk, c0, ncols = blocks[bi]
            dst = slot_view(gb)
            sc = scr[bi][:, 0:nk * ncols].rearrange("p (a b) -> p a b", a=nk)
            tile_i = gb // NBLK
            late = (bi % 4 == 0)
            if tile_i == 0 or (tile_i == 1 and late):
                src = wblk[bi][:, 0:nk * ncols]
                dst2 = WS[s][:, 0:nk * ncols]
                P.op("pool", [], [("ws", s)], lambda e, dst2=dst2, src=src: e.dma_start(out=dst2, in_=src), sem=s_wl[s])
                if (tile_i == 1) == late:
                    P.op("sp", [("ws", s)], [("scr", bi)], lambda e, dst=dst, sc=sc: e.dma_start(out=sc, in_=dst), sem=s_wb[s])
            else:
                P.op("sp", [("scr", bi)], [("ws", s)], lambda e, dst=dst, sc=sc: e.dma_start(out=dst, in_=sc), sem=s_wl[s])

        def need_block(gb):
            while loaded[0] < min(gb + NSLOT, NGB):
                emit_load(loaded[0])
                loaded[0] += 1

        bank_ctr = [0]

        def next_bank():
            b = bank_ctr[0] % 7
            bank_ctr[0] += 1
            return b

        def pe_group(gbs, parts, rhs_list, rhs_res, T, extra_reads=(), oldest=None):
            need_block(min(gbs) if oldest is None else oldest)
            b = next_bank()
            out = banks[b][:, 0:T]
            n = len(parts)
            mm = []
            for i, (gb, kl, ml) in enumerate(parts):
                lhsT = slot_view(gb)[:, kl, ml * 128:(ml + 1) * 128]
                mm.append((lhsT, rhs_list[i]))

            def emit(e, mm=mm, out=out, n=n):
                last = None
                for i, (l, r) in enumerate(mm):
                    last = e.matmul(out, lhsT=l, rhs=r, start=(i == 0), stop=(i == n - 1))
                return last
            reads = [("ws", gb % NSLOT) for gb in gbs] + list(rhs_res) + list(extra_reads)
            P.op("pe", reads, [("bank", b)], emit)
            return b

        def emit_consts(e):
            ins = [e.dma_start(out=PAR, in_=params[:, :])]
            for i in range(2):
                ins.append(e.dma_start(out=HA[:, 1 + i].rearrange("p c t -> p (c t)"), in_=stA[i]))
                ins.append(e.dma_start(out=HU[:, 1 + i].rearrange("p c t -> p (c t)"), in_=stU[i]))
                ins.append(e.dma_start(out=HF[:, 1 + i].rearrange("p c t -> p (c t)"), in_=stF[i]))
            return ins
        P.op("sp", [], ["PAR", "HA12", "HU12", "HF12"], emit_consts, sem=s_const, ninc=7)
        P.op("dve", [], ["ONES"], lambda e: e.memset(ONES[:, 0, :], 1.0 / D))
        P.op("dve", [], ["ONES"], lambda e: e.memset(ONES[:, 1, :], 1.0 / DC))
        P.op("dve", [], ["EPS"], lambda e: e.memset(EPSB, EPS))
        P.op("dve", [], ["HA0"], lambda e: e.memset(HA[:, 0].rearrange("p c t -> p (c t)"), 0.0))
        P.op("dve", [], ["HU0"], lambda e: e.memset(HU[:, 0].rearrange("p c t -> p (c t)"), 0.0))
        P.op("dve", [], ["HF0"], lambda e: e.memset(HF[:, 0].rearrange("p c t -> p (c t)"), 0.0))
        P.op("dve", [], [("abuf", c) for c in range(CC)] + [RG], lambda e: e.memset(vf(a_abuf, CC * WA), 0.0))
        P.op("dve", [], [("ubuf", 0), ("ubuf", 1), RG], lambda e: e.memset(vf(a_ubuf, 2 * WF), 0.0))

        def hres(kind, seq):
            return ("h" + kind, seq)

        def hist_reads(kind, seq):
            return [("H%s0" % kind) if seq == 0 else ("H%s12" % kind), hres(kind, seq)]

        def norm_stats(T, produce_sq, oneidx, nk, b=7):
            out = banks[b][:, 0:T]
            for k in range(nk):
                slot = k % 4
                produce_sq(k, slot)
                P.op("pe", [("sq", slot), "ONES"], [("bank", b)],
                     lambda e, k=k, slot=slot, out=out: e.matmul(out, lhsT=ONES[:, oneidx, :], rhs=SQ[:, slot, 0:T],
                                                                start=(k == 0), stop=(k == nk - 1)))
            return b

        def rstd_from_bank(b, T, dst_idx):
            P.op("act", ["EPS"], [("bank", b), ("st", dst_idx)],
                 lambda e: e.activation(out=ST[:, dst_idx, 0:T], in_=banks[b][:, 0:T], func=AF.Sqrt, bias=EPSB, scale=1.0))
            P.op("dve", [], [("st", dst_idx)],
                 lambda e: e.reciprocal(out=ST[:, dst_idx, 0:T], in_=ST[:, dst_idx, 0:T]))

        def emit_tile(ti, pts, col0):
            T = TW
            gb0 = ti * NBLK
            segs = [(0, T)]
            co, ao, uo = [0], [0], [0]
            Wc = T
            Wu = T

            P.op("dve", [], [RG], lambda e: e.memset(DUMMY, 0.0))

            def sq_x(k, slot, T=T):
                P.op("act", [("X", k)], [("sq", slot)],
                     lambda e: e.activation(out=SQ[:, slot, 0:T], in_=X[:, k, 0:T], func=AF.Square))
            if ti == 0:
                xsrc = xT[:, col0:col0 + T].rearrange("(kc p) t -> p kc t", p=128)
                P.op("sp", [], [("X", k) for k in range(KC)],
                     lambda e, xsrc=xsrc, T=T: e.dma_start(out=X[:, :, 0:T], in_=xsrc), sem=s_x)
                b = norm_stats(T, sq_x, 0, KC)
                rstd_from_bank(b, T, 0)
                for k in range(KC):
                    P.op("dve", [("X", k), ("st", 0), "PAR"], [("H", k)],
                         lambda e, k=k, T=T: e.scalar_tensor_tensor(out=H[:, k, 0:T], in0=X[:, k, 0:T], scalar=par(P_G1 + k),
                                                                   in1=ST[:, 0, 0:T], op0=ALU.mult, op1=ALU.mult))
            else:
                for k in range(KC):
                    if True:
                        P.op("act", [("gat", 2 * k), ("gat", 2 * k + 1)], [("X", k)],
                             lambda e, k=k, T=T: e.activation(out=X[:, k, 0:T], in_=XN[:, k, 0:T], func=AF.Copy))
                    else:
                        P.op("pool", [("gat", 2 * k), ("gat", 2 * k + 1)], [("X", k)],
                             lambda e, k=k, T=T: e.tensor_copy(out=X[:, k, 0:T], in_=XN[:, k, 0:T]))
            Hres = [("H", k) for k in range(KC)]
            Hrhs = [H[:, k, 0:T] for k in range(KC)]

            bgq = []

            def drain_bg(n):
                for _ in range(min(n, len(bgq))):
                    bgq.pop(0)()

            def win_group(blk, ml, oldest):
                gb = gb0 + blk
                return pe_group([gb], [(gb, k, ml) for k in range(KC)], Hrhs, Hres, T, oldest=gb0 + oldest)

            for i in range(4):
                bv, bg = seq_conf[i]
                for cc in range(2):
                    c = 2 * i + cc
                    b_val = win_group(bv, cc, bv)
                    b_gate = win_group(bg, cc, bv)
                    sl = c % 2
                    P.op("act", [RG], [("bank", b_gate), ("sg", sl)],
                         lambda e, b_gate=b_gate, sl=sl, T=T: e.activation(out=SG[:, sl, 0:T], in_=banks[b_gate][:, 0:T], func=AF.Sigmoid))
                    for si, (seq, L) in enumerate(segs):
                        P.op("dve", [("sg", sl), RG], [("bank", b_val), ("abuf", c)],
                             lambda e, b_val=b_val, sl=sl, c=c, si=si, L=L: e.tensor_tensor(
                                 out=ABUF[:, c, ao[si] + 30: ao[si] + 30 + L], in0=banks[b_val][:, co[si]:co[si] + L],
                                 in1=SG[:, sl, co[si]:co[si] + L], op=ALU.mult))

                    def bg_conv(c=c):
                        for (seq, s0, L) in pts:
                            P.op("act", hist_reads("A", seq) + [RG], [("abuf", c)],
                                 lambda e, s0=s0, seq=seq: e.activation(out=ABUF[:, c, s0:s0 + 30], in_=HA[:, seq, c, :], func=AF.Copy))
                        P.op("act", [("abuf", c), "PAR", RG], [("aconv", c)],
                             lambda e: e.activation(out=ACONV[:, c, 0:Wc], in_=ABUF[:, c, 0:Wc], func=AF.Identity,
                                                    bias=par(P_CB + c), scale=par(P_CW + c * 31)))
                        P.op("act", [("abuf", c), "PAR", RG], [("acc1", c % 2)],
                             lambda e: e.activation(out=ACC1[:, c % 2, 0:Wc], in_=ABUF[:, c, 1:1 + Wc], func=AF.Identity,
                                                    scale=par(P_CW + c * 31 + 1)))
                    bgq.insert(max(0, len(bgq) - 5), bg_conv)
                    for k in range(2, 31):
                        def bg_tap(c=c, k=k):
                            if k % 2 == 0:
                                acc, accres = ACONV[:, c, 0:Wc], ("aconv", c)
                            else:
                                acc, accres = ACC1[:, c % 2, 0:Wc], ("acc1", c % 2)
                            P.op("dve", [("abuf", c), "PAR", RG], [accres],
                                 lambda e: e.scalar_tensor_tensor(out=acc, in0=ABUF[:, c, k:k + Wc],
                                                                  scalar=par(P_CW + c * 31 + k), in1=acc,
                                                                  op0=ALU.mult, op1=ALU.add))
                        bgq.append(bg_tap)

                    def bg_hout(c=c):
                        P.op("dve", [("acc1", c % 2), RG], [("aconv", c)],
                             lambda e: e.tensor_tensor(out=ACONV[:, c, 0:Wc], in0=ACONV[:, c, 0:Wc], in1=ACC1[:, c % 2, 0:Wc], op=ALU.add))
                        for (seq, s0, L) in pts:
                            P.op("act", [("abuf", c), RG], [hres("A", seq)],
                                 lambda e, s0=s0, seq=seq, L=L: e.activation(out=HA[:, seq, c, :], in_=ABUF[:, c, s0 + L: s0 + L + 30], func=AF.Copy))
                    bgq.append(bg_hout)
                    drain_bg(6)

            for i in range(4):
                b_c, b_x, b_b = seq_sc[i]
                for cc in range(2):
                    c = 2 * i + cc
                    sl = c % 2
                    pc = win_group(b_c, cc, b_c)
                    px = win_group(b_x, cc, b_c)
                    pb = win_group(b_b, cc, b_c)
                    P.op("act", [RG], [("bank", pc), ("sct", sl)],
                         lambda e, pc=pc, sl=sl, T=T: e.activation(out=SCT[:, sl, 0:T], in_=banks[pc][:, 0:T], func=AF.Copy))
                    P.op("dve", [("sct", sl), RG], [("bank", px), ("ubuf", sl)],
                         lambda e, px=px, sl=sl: e.tensor_tensor(out=UBUF[:, sl, 2:2 + T], in0=banks[px][:, 0:T], in1=SCT[:, sl, 0:T], op=ALU.mult))
                    drain_bg(6)
                    for (seq, s0, L) in pts:
                        P.op("act", hist_reads("U", seq) + [RG], [("ubuf", sl)],
                             lambda e, sl=sl, s0=s0, seq=seq, c=c: e.activation(out=UBUF[:, sl, s0:s0 + 2], in_=HU[:, seq, c, :], func=AF.Copy))
                    P.op("act", [("ubuf", sl), "PAR", RG], [("uc", sl)],
                         lambda e, sl=sl, c=c: e.activation(out=UCV[:, sl, 0:Wu], in_=UBUF[:, sl, 0:Wu], func=AF.Identity, scale=par(P_SW + c * 3)))
                    for k in (1, 2):
                        P.op("dve", [("ubuf", sl), "PAR", RG], [("uc", sl)],
                             lambda e, sl=sl, c=c, k=k: e.scalar_tensor_tensor(out=UCV[:, sl, 0:Wu], in0=UBUF[:, sl, k:k + Wu],
                                                                              scalar=par(P_SW + c * 3 + k), in1=UCV[:, sl, 0:Wu],
                                                                              op0=ALU.mult, op1=ALU.add))
                    for (seq, s0, L) in pts:
                        P.op("act", [("ubuf", sl), RG], [hres("U", seq)],
                             lambda e, sl=sl, s0=s0, seq=seq, L=L, c=c: e.activation(out=HU[:, seq, c, :], in_=UBUF[:, sl, s0 + L: s0 + L + 2], func=AF.Copy))
                    P.op("dve", [("uc", sl), RG], [("bank", pb), ("bin", c)],
                         lambda e, pb=pb, sl=sl, c=c: e.tensor_tensor(out=BIN[:, c, 0:T], in0=banks[pb][:, 0:T], in1=UCV[:, sl, 0:T], op=ALU.mult))
                    drain_bg(6)

            def emit_ln():
                def seg_copy_act(func, c, slot):
                    for si, (seq, L) in enumerate(segs):
                        P.op("act", [("aconv", c), RG], [("sq", slot)],
                             lambda e, si=si, L=L: e.activation(out=SQ[:, slot, co[si]:co[si] + L], in_=ACONV[:, c, ao[si]:ao[si] + L], func=func))
                b_mean = norm_stats(T, lambda k, slot: seg_copy_act(AF.Copy, k, slot), 1, CC)
                b_ex2 = norm_stats(T, lambda k, slot: seg_copy_act(AF.Square, k, slot), 1, CC, b=next_bank())
                P.op("dve", [], [("bank", b_mean), ("st", 1)], lambda e, T=T, b_mean=b_mean: e.tensor_copy(out=ST[:, 1, 0:T], in_=banks[b_mean][:, 0:T]))
                P.op("dve", [("st", 1)], [("st", 2)], lambda e, T=T: e.tensor_tensor(out=ST[:, 2, 0:T], in0=ST[:, 1, 0:T], in1=ST[:, 1, 0:T], op=ALU.mult))
                P.op("dve", [], [("bank", b_ex2), ("st", 2)], lambda e, T=T, b_ex2=b_ex2: e.tensor_tensor(out=ST[:, 2, 0:T], in0=banks[b_ex2][:, 0:T], in1=ST[:, 2, 0:T], op=ALU.subtract))
                P.op("act", ["EPS"], [("st", 2)], lambda e, T=T: e.activation(out=ST[:, 2, 0:T], in_=ST[:, 2, 0:T], func=AF.Sqrt, bias=EPSB, scale=1.0))
                P.op("dve", [], [("st", 2)], lambda e, T=T: e.reciprocal(out=ST[:, 2, 0:T], in_=ST[:, 2, 0:T]))
                P.op("dve", [("st", 2)], [("st", 1)], lambda e, T=T: e.tensor_tensor(out=ST[:, 1, 0:T], in0=ST[:, 1, 0:T], in1=ST[:, 2, 0:T], op=ALU.mult))
                for c0_ in range(0, CC, 2):
                    for c in (c0_, c0_ + 1):
                        P.op("dve", [("st", 2), RG], [("aconv", c)],
                             lambda e, c=c: e.tensor_tensor(out=ACONV[:, c, 0:T], in0=ACONV[:, c, 0:T], in1=ST[:, 2, 0:T], op=ALU.mult))
                    for c in (c0_, c0_ + 1):
                        P.op("dve", [("st", 1), RG], [("aconv", c)],
                             lambda e, c=c: e.tensor_tensor(out=ACONV[:, c, 0:T], in0=ACONV[:, c, 0:T], in1=ST[:, 1, 0:T], op=ALU.subtract))
                    for c in (c0_, c0_ + 1):
                        P.op("act", [("aconv", c), "PAR", RG], [("sa", c)],
                             lambda e, c=c: e.activation(out=SA[:, c, 0:T], in_=ACONV[:, c, 0:T],
                                                         func=AF.Silu, bias=par(P_LB + c), scale=par(P_LG + c)))

            for i in range(8):
                ba, bb = seq_gate[i]
                if i == 5:
                    drain_bg(len(bgq))
                    emit_ln()
                for cc in range(2):
                    m = 2 * i + cc
                    for which, blk in ((0, ba), (1, bb)):
                        pg = win_group(blk, cc, ba)
                        gi = which * 16 + m
                        P.op("act", [RG], [("bank", pg), ("gat", gi)],
                             lambda e, pg=pg, gi=gi, T=T: e.activation(out=GAT[:, gi, 0:T], in_=banks[pg][:, 0:T], func=AF.Sigmoid))
                        drain_bg(8)

            SAres = [("sa", c) for c in range(CC)]
            SArhs = [SA[:, c, 0:T] for c in range(CC)]
            BIres = [("bin", c) for c in range(CC)]
            BIrhs = [BIN[:, c, 0:T] for c in range(CC)]
            for q in range(4):
                b1, b2 = seq_out[q]
                for ml in range(4):
                    m = 4 * q + ml
                    sl = m % 2
                    pa = pe_group([gb0 + b1], [(gb0 + b1, k, ml) for k in range(CC)], SArhs, SAres, T, [RG], oldest=gb0 + b1)
                    pb = pe_group([gb0 + b2], [(gb0 + b2, k, ml) for k in range(CC)], BIrhs, BIres, T, [RG], oldest=gb0 + b1)
                    P.op("dve", [("gat", m), RG], [("bank", pa), ("t12", sl)],
                         lambda e, pa=pa, sl=sl, m=m, T=T: e.tensor_tensor(out=T12[:, sl, 0:T], in0=banks[pa][:, 0:T], in1=GAT[:, m, 0:T], op=ALU.mult))
                    P.op("dve", [("gat", 16 + m), RG], [("bank", pb), ("t12", 2 + sl)],
                         lambda e, pb=pb, sl=sl, m=m, T=T: e.tensor_tensor(out=T12[:, 2 + sl, 0:T], in0=banks[pb][:, 0:T], in1=GAT[:, 16 + m, 0:T], op=ALU.mult))
                    P.op("dve", [("t12", sl), ("t12", 2 + sl), RG], [("mg", m)],
                         lambda e, sl=sl, m=m, T=T: e.tensor_tensor(out=MG[:, m, 0:T], in0=T12[:, sl, 0:T], in1=T12[:, 2 + sl, 0:T], op=ALU.add))

            MGres = [("mg", k) for k in range(KC)]
            MGrhs = [MG[:, k, 0:T] for k in range(KC)]
            mix_banks = []

            def evac_to_M(pm, m, gcol, slot, T=T):
                P.op("act", [], [("bank", pm), ("sq", slot)],
                     lambda e: e.activation(out=SQ[:, slot, 0:T], in_=banks[pm][:, 0:T], func=AF.Square))
                P.op("act", ["PAR"], [("bank", pm), ("M", m)],
                     lambda e: e.activation(out=M[:, m, 0:T], in_=banks[pm][:, 0:T], func=AF.Identity, scale=par(gcol + m)))

            bstat = 7

            def stat_mm(m):
                P.op("pe", [("sq", m % 4), "ONES"], [("bank", bstat)],
                     lambda e: e.matmul(banks[bstat][:, 0:T], lhsT=ONES[:, 0, :], rhs=SQ[:, m % 4, 0:T],
                                        start=(m == 0), stop=(m == KC - 1)))
            for i in range(8):
                blk = seq_wo[i]
                for ml in range(2):
                    m = 2 * i + ml
                    pm = pe_group([gb0 + blk], [(gb0 + blk, k, ml) for k in range(KC)], MGrhs, MGres, T, [RG])
                    evac_to_M(pm, m, P_G2, m % 4)
                    if m >= 2:
                        stat_mm(m - 2)
            stat_mm(KC - 2)
            stat_mm(KC - 1)
            rstd_from_bank(bstat, T, 0)
            def x1_mul(xeng, m):
                P.op(xeng, [("st", 0)], [("M", m)],
                     lambda e, m=m, T=T: e.tensor_tensor(out=M[:, m, 0:T], in0=M[:, m, 0:T], in1=ST[:, 0, 0:T], op=ALU.mult))

            def x1_add(xeng, m):
                P.op(xeng, [("M", m)], [("X", m)],
                     lambda e, m=m, T=T: e.tensor_tensor(out=X[:, m, 0:T], in0=X[:, m, 0:T], in1=M[:, m, 0:T], op=ALU.add))
            n_dve = KC if ti <= 1 else 11
            x1_mul("dve", 0)
            for m in range(n_dve):
                if m + 1 < n_dve:
                    x1_mul("dve", m + 1)
                x1_add("dve", m)
            for m in range(n_dve, KC):
                x1_mul("pool", m)
                x1_add("pool", m)

            b = norm_stats(T, sq_x, 0, KC)
            rstd_from_bank(b, T, 3)
            for k in range(KC):
                P.op("dve", [("X", k), ("st", 3), "PAR"], [("H", k)],
                     lambda e, k=k, T=T: e.scalar_tensor_tensor(out=H[:, k, 0:T], in0=X[:, k, 0:T], scalar=par(P_G3 + k),
                                                               in1=ST[:, 3, 0:T], op0=ALU.mult, op1=ALU.mult))

            P.op("dve", [], [RG], lambda e: e.memset(DUMMY, 0.0))

            has_next = ti + 1 < len(TILES)
            if has_next:
                Tn = TW
                coln = col0 + T
                xsrcn = xT[:, coln:coln + Tn].rearrange("(kc p) t -> p kc t", p=128)
                P.op("sp", [], [("gat", j) for j in range(32)],
                     lambda e: e.dma_start(out=XN[:, :, 0:Tn], in_=xsrcn), sem=s_x)

            def pro_sq(k):
                P.op("act", [("gat", 2 * k), ("gat", 2 * k + 1)], [("sq", k % 4)],
                     lambda e: e.activation(out=SQ[:, k % 4, 0:Tn], in_=XN[:, k, 0:Tn], func=AF.Square))

            def pro_mm(k):
                P.op("pe", [("sq", k % 4), "ONES"], [("bank", 7)],
                     lambda e: e.matmul(banks[7][:, 0:Tn], lhsT=ONES[:, 0, :], rhs=SQ[:, k % 4, 0:Tn],
                                        start=(k == 0), stop=(k == KC - 1)))

            ubc = [0]
            for i in range(22):
                if has_next and 10 <= i < 18:
                    pro_sq(2 * (i - 10))
                    pro_sq(2 * (i - 10) + 1)
                if has_next and 11 <= i < 19:
                    pro_mm(2 * (i - 11))
                    pro_mm(2 * (i - 11) + 1)
                if has_next and i == 19:
                    rstd_from_bank(7, Tn, 1)
                b1, b2 = seq_up[i]
                for cc in range(2):
                    j = 2 * i + cc
                    slots = []
                    for which, blk in ((0, b1), (1, b2)):
                        cidx = j + which * FC
                        pu = win_group(blk, cc, b1)
                        sl = ubc[0] % 4
                        ubc[0] += 1
                        slots.append(sl)
                        P.op("act", [RG], [("bank", pu), ("ub", sl)],
                             lambda e, sl=sl, pu=pu: e.activation(out=UB[:, sl, 2:2 + T], in_=banks[pu][:, 0:T], func=AF.Copy))
                        for (seq, s0, L) in pts:
                            P.op("act", hist_reads("F", seq) + [RG], [("ub", sl)],
                                 lambda e, sl=sl, s0=s0, seq=seq, cidx=cidx: e.activation(out=UB[:, sl, s0:s0 + 2], in_=HF[:, seq, cidx, :], func=AF.Copy))
                            P.op("act", [], [("bank", pu), hres("F", seq)],
                                 lambda e, s0=s0, seq=seq, L=L, pu=pu, cidx=cidx: e.activation(out=HF[:, seq, cidx, :], in_=banks[pu][:, s0 + L - 2:s0 + L], func=AF.Copy))
                        P.op("dve", [("ub", sl), "PAR", RG], [("cv", sl)],
                             lambda e, sl=sl, cidx=cidx: e.tensor_scalar(out=CV[:, sl, 0:Wu], in0=UB[:, sl, 2:2 + Wu], scalar1=par(P_FW + cidx * 3 + 2),
                                                                         scalar2=None, op0=ALU.mult))
                        for k in (1, 0):
                            P.op("dve", [("ub", sl), "PAR", RG], [("cv", sl)],
                                 lambda e, sl=sl, cidx=cidx, k=k: e.scalar_tensor_tensor(out=CV[:, sl, 0:Wu], in0=UB[:, sl, k:k + Wu],
                                                                                        scalar=par(P_FW + cidx * 3 + k), in1=CV[:, sl, 0:Wu],
                                                                                        op0=ALU.mult, op1=ALU.add))
                    sg_, sv_ = slots
                    ss = j % 2
                    for si, (seq, L) in enumerate(segs):
                        P.op("act", [("cv", sg_), RG], [("sil", ss)],
                             lambda e, sg_=sg_, ss=ss, si=si, L=L: e.activation(out=SIL[:, ss, co[si]:co[si] + L], in_=CV[:, sg_, uo[si]:uo[si] + L], func=AF.Silu))
                        P.op("dve", [("sil", ss), ("cv", sv_), RG], [("g", j)],
                             lambda e, sv_=sv_, ss=ss, si=si, L=L, j=j: e.tensor_tensor(out=G[:, j, co[si]:co[si] + L], in0=SIL[:, ss, co[si]:co[si] + L],
                                                                                       in1=CV[:, sv_, uo[si]:uo[si] + L], op=ALU.mult))

            if has_next:
                for k in range(KC):
                    P.op("dve", [("gat", 2 * k), ("gat", 2 * k + 1), ("st", 1), "PAR"], [("H", k)],
                         lambda e, k=k: e.scalar_tensor_tensor(out=H[:, k, 0:Tn], in0=XN[:, k, 0:Tn], scalar=par(P_G1 + k),
                                                              in1=ST[:, 1, 0:Tn], op0=ALU.mult, op1=ALU.mult))

            Gres = [("g", j) for j in range(FC)]
            bstat = 7
            for m in range(16):
                b1, b2 = seq_dn[m]
                parts = [(gb0 + b1, k, 0) for k in range(22)] + [(gb0 + b2, k, 0) for k in range(22)]
                pf = pe_group([gb0 + b1, gb0 + b2], parts, [G[:, j, 0:T] for j in range(FC)], Gres, T, [RG])
                evac_to_M(pf, m, P_G4, m % 4)
                if m >= 1:
                    stat_mm(m - 1)
            stat_mm(KC - 1)
            rstd_from_bank(bstat, T, 0)
            yeng = "pool" if (has_next and ti >= 1) else "dve"
            for m in range(KC):
                P.op(yeng, [("st", 0)], [("M", m)],
                     lambda e, m=m, T=T: e.tensor_tensor(out=M[:, m, 0:T], in0=M[:, m, 0:T], in1=ST[:, 0, 0:T], op=ALU.mult))
                P.op(yeng, [("X", m)], [("M", m)],
                     lambda e, m=m, T=T: e.tensor_tensor(out=M[:, m, 0:T], in0=M[:, m, 0:T], in1=X[:, m, 0:T], op=ALU.add))
            ydst = yT[:, col0:col0 + T].rearrange("(kc p) t -> p kc t", p=128)
            P.op("pool", [("M", m) for m in range(KC)], ["Y"],
                 lambda e, ydst=ydst, T=T: e.dma_start(out=ydst, in_=M[:, :, 0:T]), sem=s_y)

        col0 = 0
        for ti, pts in enumerate(TILES):
            emit_tile(ti, pts, col0)
            col0 += TW

        def emit_outs(e):
            ins = []
            for s in range(3):
                ins.append(e.dma_start(out=oA[s], in_=HA[:, s].rearrange("p c t -> p (c t)")))
                ins.append(e.dma_start(out=oU[s], in_=HU[:, s].rearrange("p c t -> p (c t)")))
                ins.append(e.dma_start(out=oF[s], in_=HF[:, s].rearrange("p c t -> p (c t)")))
            return ins
        allh = [hres(k, s) for k in "AUF" for s in range(3)] + ["HA0", "HU0", "HF0", "HA12", "HU12", "HF12"]
        P.op("sp", allh, ["OUT"], emit_outs, sem=s_out, ninc=9)
        final_sems = [s_y, s_out, s_x, s_const] + s_wl + s_wb

        @block.tensor
        def _(e):
            P.run("pe", e)

        @block.scalar
        def _(e):
            P.run("act", e)

        @block.vector
        def _(e):
            P.run("dve", e)

        @block.gpsimd
        def _(e):
            P.run("pool", e)

        @block.sync
        def _(e):
            P.run("sp", e)
            for s in final_sems:
                if s.val > 0:
                    e.wait_ge(s.h, s.val)
            for k in ("pe", "act", "dve"):
                e.wait_ge(csem[k].h, csem[k].val)
    return nc


_NC_CACHE = {}


def _get_nc():
    if "nc" not in _NC_CACHE:
        _NC_CACHE["nc"] = build_nc()
    return _NC_CACHE["nc"]


def _pack_params(g1, g2, g3, g4, cb, lg, lb, cw, sw, fw):
    p = np.zeros((128, NPAR), np.float32)
    p[:, P_G1:P_G1 + 16] = g1.reshape(16, 128).T
    p[:, P_G2:P_G2 + 16] = g2.reshape(16, 128).T
    p[:, P_G3:P_G3 + 16] = g3.reshape(16, 128).T
    p[:, P_G4:P_G4 + 16] = g4.reshape(16, 128).T
    p[:, P_CB:P_CB + 8] = cb.reshape(8, 128).T
    p[:, P_LG:P_LG + 8] = lg.reshape(8, 128).T
    p[:, P_LB:P_LB + 8] = lb.reshape(8, 128).T
    p[:, P_CW:P_CW + 248] = cw.T.reshape(8, 128, 31).transpose(1, 0, 2).reshape(128, 248)
    p[:, P_SW:P_SW + 24] = sw.T.reshape(8, 128, 3).transpose(1, 0, 2).reshape(128, 24)
    p[:, P_FW:P_FW + 264] = fw.T.reshape(88, 128, 3).transpose(1, 0, 2).reshape(128, 264)
    return p


def _state_to_dev(st):
    w1, C = st.shape
    return np.ascontiguousarray(st.T.reshape(C // 128, 128, w1).transpose(1, 0, 2).reshape(128, -1))


def _state_from_dev(a, w1):
    ncn = a.shape[1] // w1
    return np.ascontiguousarray(a.reshape(128, ncn, w1).transpose(2, 1, 0).reshape(w1, ncn * 128))


def kernel(x_prompt, x_sample, state_conf_conv, state_sconv, state_ffn_conv, meta_tokens,
           g_pre_mix, w_in, conf_conv_w, conf_conv_b, conf_ln_g, conf_ln_b, w_conf_out,
           sconv_w, w_sconv_out, w_o, g_post_mix, g_pre_ffn, w_up, ffn_conv_w, w_down, g_post_ffn):
    f = lambda a: np.asarray(a, dtype=np.float32)
    x_prompt, x_sample, meta_tokens = f(x_prompt), f(x_sample), f(meta_tokens)
    sA, sU, sF = f(state_conf_conv)[0], f(state_sconv)[0], f(state_ffn_conv)[0]
    params = _pack_params(f(g_pre_mix)[0], f(g_post_mix)[0], f(g_pre_ffn)[0], f(g_post_ffn)[0], f(conf_conv_b)[0],
                          f(conf_ln_g)[0], f(conf_ln_b)[0], f(conf_conv_w)[0], f(sconv_w)[0], f(ffn_conv_w)[0])
    wsrc = {"w_in": f(w_in)[0], "w_conf_out": f(w_conf_out)[0], "w_sconv_out": f(w_sconv_out)[0],
            "w_o": f(w_o)[0], "w_up": f(w_up)[0], "w_down": f(w_down)[0]}
    blocks = make_blocks()[0]
    wblk = np.zeros((len(blocks), 128, SLOT_ELEMS), np.float32)
    for bi, (wn, r0, nk, c0, ncols) in enumerate(blocks):
        blk = wsrc[wn][r0:r0 + nk * 128, c0:c0 + ncols]
        wblk[bi, :, :nk * ncols] = blk.reshape(nk, 128, ncols).transpose(1, 0, 2).reshape(128, nk * ncols)
    shared = {"params": params, "wblk": wblk}
    in_maps = []
    for c in range(NCORES):
        b, half = c // 2, c % 2
        ext = np.concatenate([meta_tokens, x_prompt[b]], axis=0)
        part = ext[0:HALF] if half == 0 else ext[EXT - HALF:EXT]
        zg = np.zeros((GAP, D), np.float32)
        rows = np.concatenate([part, zg, x_sample[2 * c], zg, x_sample[2 * c + 1]], axis=0)
        m = dict(shared)
        m["xT"] = np.ascontiguousarray(rows.T)
        m["stA"] = np.stack([_state_to_dev(sA[2 * c + i]) for i in range(2)])
        m["stU"] = np.stack([_state_to_dev(sU[2 * c + i]) for i in range(2)])
        m["stF"] = np.stack([_state_to_dev(sF[2 * c + i]) for i in range(2)])
        in_maps.append(m)
    nc = _get_nc()
    res = run_bass_kernel_spmd(nc, in_maps, core_ids=list(range(NCORES)))
    B = x_prompt.shape[0]
    y_prompt = np.zeros((B, SEQ, D), np.float32)
    y_sample = np.zeros((2 * NCORES, SL, D), np.float32)
    ncp = np.zeros((1, B, 30, DC), np.float32)
    nsp = np.zeros((1, B, 2, DC), np.float32)
    nfp = np.zeros((1, B, 2, 2 * DFF), np.float32)
    ncs = np.zeros((1, 2 * NCORES, 30, DC), np.float32)
    nss = np.zeros((1, 2 * NCORES, 2, DC), np.float32)
    nfs = np.zeros((1, 2 * NCORES, 2, 2 * DFF), np.float32)
    first = HALF - NMETA
    for c in range(NCORES):
        r = res.results[c]
        b, half = c // 2, c % 2
        y = np.asarray(r["yT"]).T
        if half == 0:
            y_prompt[b, 0:first] = y[NMETA:HALF]
        else:
            y_prompt[b, first:SEQ] = y[HALO:HALF]
            ncp[0, b] = _state_from_dev(np.asarray(r["oA"])[0], 30)
            nsp[0, b] = _state_from_dev(np.asarray(r["oU"])[0], 2)
            nfp[0, b] = _state_from_dev(np.asarray(r["oF"])[0], 2)
        for i in range(2):
            y_sample[2 * c + i] = y[HALF + GAP + (GAP + SL) * i: HALF + (GAP + SL) * (i + 1)]
            ncs[0, 2 * c + i] = _state_from_dev(np.asarray(r["oA"])[1 + i], 30)
            nss[0, 2 * c + i] = _state_from_dev(np.asarray(r["oU"])[1 + i], 2)
            nfs[0, 2 * c + i] = _state_from_dev(np.asarray(r["oF"])[1 + i], 2)
    return (y_prompt, y_sample, ncp, nsp, nfp, ncs, nss, nfs)
```

```python
import numpy as np
import concourse.bass as bass
import concourse.mybir as mybir
from concourse.bass_utils import run_bass_kernel_spmd

F32 = mybir.dt.float32
BF16 = mybir.dt.bfloat16
ALU = mybir.AluOpType
AF = mybir.ActivationFunctionType

D = 2048
KC = 16
DC = 1024
CC = 8
DFF = 5632
FC = 44
UC = 88
DIN = 9216
NCORES = 8
NMETA = 16
SEQ = 4096
EXT = SEQ + NMETA
HALF = 2072
HALO = 32
SL = 32
GAP = 36
NT = HALF + 2 * (GAP + SL)
TW = 368
EPS = 1e-6
NSLOT = 5
SLOT_ELEMS = 4096

P_G1, P_G2, P_G3, P_G4 = 0, 16, 32, 48
P_CB, P_LG, P_LB = 64, 72, 80
P_CW = 88
P_SW = P_CW + 8 * 31
P_FW = P_SW + 8 * 3
NPAR = P_FW + 88 * 3

LASTP = HALF - 5 * TW
TILES = [[(0, 0, TW)] for _ in range(5)] + [[(0, 0, LASTP), (1, LASTP + GAP, SL), (2, LASTP + 2 * GAP + SL, SL)]]
assert LASTP + 2 * (GAP + SL) == TW and 6 * TW == NT and GAP >= 30


def make_blocks():
    blocks = []

    def add_block(w, r0, nk, c0, ncols):
        blocks.append((w, r0, nk, c0, ncols))
        return len(blocks) - 1

    seq_conf = []
    for i in range(4):
        bv = add_block("w_in", 0, KC, 256 * i, 256)
        bg = add_block("w_in", 0, KC, 1024 + 256 * i, 256)
        seq_conf.append((bv, bg))
    seq_sc = []
    for i in range(4):
        b_c = add_block("w_in", 0, KC, 3072 + 256 * i, 256)
        b_x = add_block("w_in", 0, KC, 4096 + 256 * i, 256)
        b_b = add_block("w_in", 0, KC, 2048 + 256 * i, 256)
        seq_sc.append((b_c, b_x, b_b))
    seq_gate = []
    for i in range(8):
        ba = add_block("w_in", 0, KC, 5120 + 256 * i, 256)
        bb = add_block("w_in", 0, KC, 7168 + 256 * i, 256)
        seq_gate.append((ba, bb))
    seq_out = []
    for q in range(4):
        b1 = add_block("w_conf_out", 0, CC, 512 * q, 512)
        b2 = add_block("w_sconv_out", 0, CC, 512 * q, 512)
        seq_out.append((b1, b2))
    seq_wo = [add_block("w_o", 0, KC, 256 * i, 256) for i in range(8)]
    seq_up = []
    for i in range(22):
        b1 = add_block("w_up", 0, KC, 256 * i, 256)
        b2 = add_block("w_up", 0, KC, DFF + 256 * i, 256)
        seq_up.append((b1, b2))
    seq_dn = []
    for m in range(16):
        b1 = add_block("w_down", 0, 22, 128 * m, 128)
        b2 = add_block("w_down", 22 * 128, 22, 128 * m, 128)
        seq_dn.append((b1, b2))
    return blocks, seq_conf, seq_sc, seq_gate, seq_out, seq_wo, seq_up, seq_dn


class Sem:
    def __init__(self, h, step):
        self.h = h
        self.step = step
        self.val = 0


class Prog:
    STREAMS = ("pe", "act", "dve", "pool", "sp")

    def __init__(self, csem):
        self.csem = csem
        self.q = {s: [] for s in self.STREAMS}
        self.waited = {s: {} for s in self.STREAMS}
        self.lastw = {}
        self.readers = {}

    def op(self, stream, reads, writes, emit, sem=None, ninc=1):
        sem = sem or self.csem[stream]
        need = {}
        for r in reads:
            lw = self.lastw.get(r)
            if lw and need.get(lw[0], 0) < lw[1]:
                need[lw[0]] = lw[1]
        for w in writes:
            lw = self.lastw.get(w)
            if lw and need.get(lw[0], 0) < lw[1]:
                need[lw[0]] = lw[1]
            for s, v in self.readers.get(w, {}).items():
                if need.get(s, 0) < v:
                    need[s] = v
        wl = []
        for s, v in need.items():
            if stream == "pe" and s is self.csem["pe"]:
                continue
            if self.waited[stream].get(s, 0) >= v:
                continue
            self.waited[stream][s] = v
            wl.append((s, v))
        sem.val += sem.step * ninc
        myval = sem.val
        self.q[stream].append((wl, emit, sem, ninc))
        for r in reads:
            d = self.readers.setdefault(r, {})
            if d.get(sem, 0) < myval:
                d[sem] = myval
        for w in writes:
            self.lastw[w] = (sem, myval)
            self.readers[w] = {}

    def run(self, stream, e):
        for wl, emit, sem, ninc in self.q[stream]:
            for s, v in wl:
                e.wait_ge(s.h, v)
            res = emit(e)
            ins = res if isinstance(res, list) else [res]
            assert len(ins) == ninc
            for i in ins:
                i.then_inc(sem.h, sem.step)


def build_nc():
    nc = bass.Bass("TRN2", target_bir_lowering=False)
    dt_in = lambda name, shape: nc.dram_tensor(name, shape, F32, kind="ExternalInput").ap()
    xT = dt_in("xT", [D, NT])
    params = dt_in("params", [128, NPAR])
    stA = dt_in("stA", [2, 128, CC * 30])
    stU = dt_in("stU", [2, 128, CC * 2])
    stF = dt_in("stF", [2, 128, UC * 2])
    yT = nc.dram_tensor("yT", [D, NT], F32, kind="ExternalOutput").ap()
    oA = nc.dram_tensor("oA", [3, 128, CC * 30], F32, kind="ExternalOutput").ap()
    oU = nc.dram_tensor("oU", [3, 128, CC * 2], F32, kind="ExternalOutput").ap()
    oF = nc.dram_tensor("oF", [3, 128, UC * 2], F32, kind="ExternalOutput").ap()
    NBLK = 128
    wblk = dt_in("wblk", [NBLK, 128, SLOT_ELEMS])
    scr = nc.dram_tensor("wscr", [NBLK, 128, SLOT_ELEMS], BF16, kind="Internal").ap()

    off = [0]
    plan = {}

    def alloc(name, nbytes):
        nb = (nbytes + 31) // 32 * 32
        plan[name] = off[0]
        off[0] += nb
        return plan[name]

    WA = TW + 30
    WF = TW + 2
    a_X = alloc("X", KC * TW * 4)
    a_H = alloc("H", KC * TW * 2)
    a_M = alloc("M", KC * TW * 4)
    a_sq = alloc("sq", 4 * TW * 2)
    a_st = alloc("st", 4 * TW * 4)
    a_par = alloc("par", NPAR * 4)
    a_hA = alloc("hA", 3 * CC * 30 * 4)
    a_hU = alloc("hU", 3 * CC * 2 * 4)
    a_hF = alloc("hF", 3 * UC * 2 * 4)
    a_one = alloc("ones", 2 * 128 * 2)
    a_eps = alloc("eps", 32)
    a_dum = alloc("dummy", 32)
    a_acc1 = alloc("acc1", 2 * (WA - 30) * 4)
    a_ws = alloc("ws", NSLOT * SLOT_ELEMS * 2)
    a_reg = off[0]
    a_abuf = alloc("abuf", CC * WA * 4)
    a_aconv = alloc("aconv", CC * (WA - 30) * 4)
    a_sa = alloc("sa", CC * TW * 2)
    a_sg = alloc("sg", 2 * TW * 4)
    a_sct = alloc("sct", 2 * TW * 4)
    a_ubuf = alloc("ubuf", 2 * WF * 4)
    a_uc = alloc("uc", 2 * TW * 4)
    a_bin = alloc("bin", CC * TW * 2)
    a_gat = alloc("gat", 32 * TW * 2)
    a_t12 = alloc("t12", 4 * TW * 4)
    a_mg = alloc("mg", KC * TW * 2)
    endB = off[0]
    off[0] = a_reg
    a_ub = alloc("ub", 4 * WF * 4)
    a_cv = alloc("cv", 4 * TW * 4)
    a_sil = alloc("sil", 2 * TW * 4)
    a_g = alloc("g", FC * TW * 2)
    endC = off[0]
    total = max(endB, endC)
    assert total <= 212000, total

    arena = nc.alloc_sbuf_tensor("arena", [128, total // 4 + 8], F32)
    arena_bf = arena.bitcast(BF16)

    def vf(a, n):
        return arena[:, a // 4: a // 4 + n]

    def vb(a, n):
        return arena_bf[:, a // 2: a // 2 + n]

    X = vf(a_X, KC * TW).rearrange("p (a b) -> p a b", a=KC)
    H = vb(a_H, KC * TW).rearrange("p (a b) -> p a b", a=KC)
    M = vf(a_M, KC * TW).rearrange("p (a b) -> p a b", a=KC)
    SQ = vb(a_sq, 4 * TW).rearrange("p (a b) -> p a b", a=4)
    ST = vf(a_st, 4 * TW).rearrange("p (a b) -> p a b", a=4)
    PAR = vf(a_par, NPAR)
    HA = vf(a_hA, 3 * CC * 30).rearrange("p (s c t) -> p s c t", s=3, c=CC)
    HU = vf(a_hU, 3 * CC * 2).rearrange("p (s c t) -> p s c t", s=3, c=CC)
    HF = vf(a_hF, 3 * UC * 2).rearrange("p (s c t) -> p s c t", s=3, c=UC)
    ONES = vb(a_one, 256).rearrange("p (a b) -> p a b", a=2)
    EPSB = vf(a_eps, 1)
    DUMMY = vf(a_dum, 1)
    ACC1 = vf(a_acc1, 2 * (WA - 30)).rearrange("p (a b) -> p a b", a=2)
    WS = [vb(a_ws + s * SLOT_ELEMS * 2, SLOT_ELEMS) for s in range(NSLOT)]
    ABUF = vf(a_abuf, CC * WA).rearrange("p (a b) -> p a b", a=CC)
    ACONV = vf(a_aconv, CC * (WA - 30)).rearrange("p (a b) -> p a b", a=CC)
    SA = vb(a_sa, CC * TW).rearrange("p (a b) -> p a b", a=CC)
    SG = vf(a_sg, 2 * TW).rearrange("p (a b) -> p a b", a=2)
    SCT = vf(a_sct, 2 * TW).rearrange("p (a b) -> p a b", a=2)
    UBUF = vf(a_ubuf, 2 * WF).rearrange("p (a b) -> p a b", a=2)
    UCV = vf(a_uc, 2 * TW).rearrange("p (a b) -> p a b", a=2)
    BIN = vb(a_bin, CC * TW).rearrange("p (a b) -> p a b", a=CC)
    GAT = vb(a_gat, 32 * TW).rearrange("p (a b) -> p a b", a=32)
    XN = vf(a_gat, KC * TW).rearrange("p (a b) -> p a b", a=KC)
    T12 = vf(a_t12, 4 * TW).rearrange("p (a b) -> p a b", a=4)
    MG = vb(a_mg, KC * TW).rearrange("p (a b) -> p a b", a=KC)
    UB = vf(a_ub, 4 * WF).rearrange("p (a b) -> p a b", a=4)
    CV = vf(a_cv, 4 * TW).rearrange("p (a b) -> p a b", a=4)
    SIL = vf(a_sil, 2 * TW).rearrange("p (a b) -> p a b", a=2)
    G = vb(a_g, FC * TW).rearrange("p (a b) -> p a b", a=FC)

    def par(col):
        return PAR[:, col:col + 1]

    banks = [nc.alloc_psum_tensor("bank%d" % i, [128, 512], F32) for i in range(8)]

    sem_names = ["pe", "act", "dve", "pool", "x", "y", "const", "out"] + \
                ["wl%d" % i for i in range(NSLOT)] + ["wb%d" % i for i in range(NSLOT)]
    import contextlib
    with contextlib.ExitStack() as es:
        sh = {n: es.enter_context(nc.semaphore("s_" + n)) for n in sem_names}
        block = es.enter_context(nc.Block())
        csem = {k: Sem(sh[k], 1) for k in ("pe", "act", "dve", "pool")}
        s_x, s_y, s_const, s_out = (Sem(sh[k], 16) for k in ("x", "y", "const", "out"))
        s_wl = [Sem(sh["wl%d" % i], 16) for i in range(NSLOT)]
        s_wb = [Sem(sh["wb%d" % i], 16) for i in range(NSLOT)]
        P = Prog(csem)
        RG = "REGION"

        blocks, seq_conf, seq_sc, seq_gate, seq_out, seq_wo, seq_up, seq_dn = make_blocks()
        assert len(blocks) == NBLK
        NGB = NBLK * len(TILES)
        loaded = [0]

        def slot_view(gb):
            w, r0, nk, c0, ncols = blocks[gb % NBLK]
            return WS[gb % NSLOT][:, 0:nk * ncols].rearrange("p (a b) -> p a b", a=nk)

        def emit_load(gb):
            bi = gb % NBLK
            s = gb % NSLOT
            w, r0, nk, c0, ncols = blocks[bi]
            dst = slot_view(gb)
            sc = scr[bi][:, 0:nk * ncols].rearrange("p (a b) -> p a b", a=nk)
            tile_i = gb // NBLK
            late = (bi % 4 == 0)
            if tile_i == 0 or (tile_i == 1 and late):
                src = wblk[bi][:, 0:nk * ncols]
                dst2 = WS[s][:, 0:nk * ncols]
                P.op("pool", [], [("ws", s)], lambda e, dst2=dst2, src=src: e.dma_start(out=dst2, in_=src), sem=s_wl[s])
                if (tile_i == 1) == late:
                    P.op("sp", [("ws", s)], [("scr", bi)], lambda e, dst=dst, sc=sc: e.dma_start(out=sc, in_=dst), sem=s_wb[s])
            else:
                P.op("sp", [("scr", bi)], [("ws", s)], lambda e, dst=dst, sc=sc: e.dma_start(out=dst, in_=sc), sem=s_wl[s])

        def need_block(gb):
            while loaded[0] < min(gb + NSLOT, NGB):
                emit_load(loaded[0])
                loaded[0] += 1

        bank_ctr = [0]

        def next_bank():
            b = bank_ctr[0] % 7
            bank_ctr[0] += 1
            return b

        def pe_group(gbs, parts, rhs_list, rhs_res, T, extra_reads=(), oldest=None):
            need_block(min(gbs) if oldest is None else oldest)
            b = next_bank()
            out = banks[b][:, 0:T]
            n = len(parts)
            mm = []
            for i, (gb, kl, ml) in enumerate(parts):
                lhsT = slot_view(gb)[:, kl, ml * 128:(ml + 1) * 128]
                mm.append((lhsT, rhs_list[i]))

            def emit(e, mm=mm, out=out, n=n):
                last = None
                for i, (l, r) in enumerate(mm):
                    last = e.matmul(out, lhsT=l, rhs=r, start=(i == 0), stop=(i == n - 1))
                return last
            reads = [("ws", gb % NSLOT) for gb in gbs] + list(rhs_res) + list(extra_reads)
            P.op("pe", reads, [("bank", b)], emit)
            return b

        def emit_consts(e):
            ins = [e.dma_start(out=PAR, in_=params[:, :])]
            for i in range(2):
                ins.append(e.dma_start(out=HA[:, 1 + i].rearrange("p c t -> p (c t)"), in_=stA[i]))
                ins.append(e.dma_start(out=HU[:, 1 + i].rearrange("p c t -> p (c t)"), in_=stU[i]))
                ins.append(e.dma_start(out=HF[:, 1 + i].rearrange("p c t -> p (c t)"), in_=stF[i]))
            return ins
        P.op("sp", [], ["PAR", "HA12", "HU12", "HF12"], emit_consts, sem=s_const, ninc=7)
        P.op("dve", [], ["ONES"], lambda e: e.memset(ONES[:, 0, :], 1.0 / D))
        P.op("dve", [], ["ONES"], lambda e: e.memset(ONES[:, 1, :], 1.0 / DC))
        P.op("dve", [], ["EPS"], lambda e: e.memset(EPSB, EPS))
        P.op("dve", [], ["HA0"], lambda e: e.memset(HA[:, 0].rearrange("p c t -> p (c t)"), 0.0))
        P.op("dve", [], ["HU0"], lambda e: e.memset(HU[:, 0].rearrange("p c t -> p (c t)"), 0.0))
        P.op("dve", [], ["HF0"], lambda e: e.memset(HF[:, 0].rearrange("p c t -> p (c t)"), 0.0))
        P.op("dve", [], [("abuf", c) for c in range(CC)] + [RG], lambda e: e.memset(vf(a_abuf, CC * WA), 0.0))
        P.op("dve", [], [("ubuf", 0), ("ubuf", 1), RG], lambda e: e.memset(vf(a_ubuf, 2 * WF), 0.0))

        def hres(kind, seq):
            return ("h" + kind, seq)

        def hist_reads(kind, seq):
            return [("H%s0" % kind) if seq == 0 else ("H%s12" % kind), hres(kind, seq)]

        def norm_stats(T, produce_sq, oneidx, nk, b=7):
            out = banks[b][:, 0:T]
            for k in range(nk):
                slot = k % 4
                produce_sq(k, slot)
                P.op("pe", [("sq", slot), "ONES"], [("bank", b)],
                     lambda e, k=k, slot=slot, out=out: e.matmul(out, lhsT=ONES[:, oneidx, :], rhs=SQ[:, slot, 0:T],
                                                                start=(k == 0), stop=(k == nk - 1)))
            return b

        def rstd_from_bank(b, T, dst_idx):
            P.op("act", ["EPS"], [("bank", b), ("st", dst_idx)],
                 lambda e: e.activation(out=ST[:, dst_idx, 0:T], in_=banks[b][:, 0:T], func=AF.Sqrt, bias=EPSB, scale=1.0))
            P.op("dve", [], [("st", dst_idx)],
                 lambda e: e.reciprocal(out=ST[:, dst_idx, 0:T], in_=ST[:, dst_idx, 0:T]))

        def emit_tile(ti, pts, col0):
            T = TW
            gb0 = ti * NBLK
            segs = [(0, T)]
            co, ao, uo = [0], [0], [0]
            Wc = T
            Wu = T

            P.op("dve", [], [RG], lambda e: e.memset(DUMMY, 0.0))

            def sq_x(k, slot, T=T):
                P.op("act", [("X", k)], [("sq", slot)],
                     lambda e: e.activation(out=SQ[:, slot, 0:T], in_=X[:, k, 0:T], func=AF.Square))
            if ti == 0:
                xsrc = xT[:, col0:col0 + T].rearrange("(kc p) t -> p kc t", p=128)
                P.op("sp", [], [("X", k) for k in range(KC)],
                     lambda e, xsrc=xsrc, T=T: e.dma_start(out=X[:, :, 0:T], in_=xsrc), sem=s_x)
                b = norm_stats(T, sq_x, 0, KC)
                rstd_from_bank(b, T, 0)
                for k in range(KC):
                    P.op("dve", [("X", k), ("st", 0), "PAR"], [("H", k)],
                         lambda e, k=k, T=T: e.scalar_tensor_tensor(out=H[:, k, 0:T], in0=X[:, k, 0:T], scalar=par(P_G1 + k),
                                                                   in1=ST[:, 0, 0:T], op0=ALU.mult, op1=ALU.mult))
            else:
                pass

            def emit_xcopy():
                for k in range(KC):
                    P.op("act", [("gat", 2 * k), ("gat", 2 * k + 1)], [("X", k)],
                         lambda e, k=k, T=T: e.activation(out=X[:, k, 0:T], in_=XN[:, k, 0:T], func=AF.Copy))
            Hres = [("H", k) for k in range(KC)]
            Hrhs = [H[:, k, 0:T] for k in range(KC)]

            bgq = []

            def drain_bg(n):
                for _ in range(min(n, len(bgq))):
                    bgq.pop(0)()

            def win_group(blk, ml, oldest):
                gb = gb0 + blk
                return pe_group([gb], [(gb, k, ml) for k in range(KC)], Hrhs, Hres, T, oldest=gb0 + oldest)

            for i in range(4):
                bv, bg = seq_conf[i]
                for cc in range(2):
                    c = 2 * i + cc
                    b_val = win_group(bv, cc, bv)
                    b_gate = win_group(bg, cc, bv)
                    sl = c % 2
                    P.op("act", [RG], [("bank", b_gate), ("sg", sl)],
                         lambda e, b_gate=b_gate, sl=sl, T=T: e.activation(out=SG[:, sl, 0:T], in_=banks[b_gate][:, 0:T], func=AF.Sigmoid))
                    for si, (seq, L) in enumerate(segs):
                        P.op("dve", [("sg", sl), RG], [("bank", b_val), ("abuf", c)],
                             lambda e, b_val=b_val, sl=sl, c=c, si=si, L=L: e.tensor_tensor(
                                 out=ABUF[:, c, ao[si] + 30: ao[si] + 30 + L], in0=banks[b_val][:, co[si]:co[si] + L],
                                 in1=SG[:, sl, co[si]:co[si] + L], op=ALU.mult))

                    def bg_conv(c=c):
                        for (seq, s0, L) in pts:
                            P.op("act", hist_reads("A", seq) + [RG], [("abuf", c)],
                                 lambda e, s0=s0, seq=seq: e.activation(out=ABUF[:, c, s0:s0 + 30], in_=HA[:, seq, c, :], func=AF.Copy))
                        P.op("act", [("abuf", c), "PAR", RG], [("aconv", c)],
                             lambda e: e.activation(out=ACONV[:, c, 0:Wc], in_=ABUF[:, c, 0:Wc], func=AF.Identity,
                                                    bias=par(P_CB + c), scale=par(P_CW + c * 31)))
                        P.op("act", [("abuf", c), "PAR", RG], [("acc1", c % 2)],
                             lambda e: e.activation(out=ACC1[:, c % 2, 0:Wc], in_=ABUF[:, c, 1:1 + Wc], func=AF.Identity,
                                                    scale=par(P_CW + c * 31 + 1)))
                    bgq.insert(max(0, len(bgq) - 5), bg_conv)
                    for k in range(2, 31):
                        def bg_tap(c=c, k=k):
                            if k % 2 == 0:
                                acc, accres = ACONV[:, c, 0:Wc], ("aconv", c)
                            else:
                                acc, accres = ACC1[:, c % 2, 0:Wc], ("acc1", c % 2)
                            P.op("dve", [("abuf", c), "PAR", RG], [accres],
                                 lambda e: e.scalar_tensor_tensor(out=acc, in0=ABUF[:, c, k:k + Wc],
                                                                  scalar=par(P_CW + c * 31 + k), in1=acc,
                                                                  op0=ALU.mult, op1=ALU.add))
                        bgq.append(bg_tap)

                    def bg_hout(c=c):
                        P.op("dve", [("acc1", c % 2), RG], [("aconv", c)],
                             lambda e: e.tensor_tensor(out=ACONV[:, c, 0:Wc], in0=ACONV[:, c, 0:Wc], in1=ACC1[:, c % 2, 0:Wc], op=ALU.add))
                        for (seq, s0, L) in pts:
                            P.op("act", [("abuf", c), RG], [hres("A", seq)],
                                 lambda e, s0=s0, seq=seq, L=L: e.activation(out=HA[:, seq, c, :], in_=ABUF[:, c, s0 + L: s0 + L + 30], func=AF.Copy))
                    bgq.append(bg_hout)
                    drain_bg(6)

            for i in range(4):
                b_c, b_x, b_b = seq_sc[i]
                for cc in range(2):
                    c = 2 * i + cc
                    sl = c % 2
                    pc = win_group(b_c, cc, b_c)
                    px = win_group(b_x, cc, b_c)
                    pb = win_group(b_b, cc, b_c)
                    P.op("act", [RG], [("bank", pc), ("sct", sl)],
                         lambda e, pc=pc, sl=sl, T=T: e.activation(out=SCT[:, sl, 0:T], in_=banks[pc][:, 0:T], func=AF.Copy))
                    P.op("dve", [("sct", sl), RG], [("bank", px), ("ubuf", sl)],
                         lambda e, px=px, sl=sl: e.tensor_tensor(out=UBUF[:, sl, 2:2 + T], in0=banks[px][:, 0:T], in1=SCT[:, sl, 0:T], op=ALU.mult))
                    drain_bg(6)
                    for (seq, s0, L) in pts:
                        P.op("act", hist_reads("U", seq) + [RG], [("ubuf", sl)],
                             lambda e, sl=sl, s0=s0, seq=seq, c=c: e.activation(out=UBUF[:, sl, s0:s0 + 2], in_=HU[:, seq, c, :], func=AF.Copy))
                    P.op("act", [("ubuf", sl), "PAR", RG], [("uc", sl)],
                         lambda e, sl=sl, c=c: e.activation(out=UCV[:, sl, 0:Wu], in_=UBUF[:, sl, 0:Wu], func=AF.Identity, scale=par(P_SW + c * 3)))
                    for k in (1, 2):
                        P.op("dve", [("ubuf", sl), "PAR", RG], [("uc", sl)],
                             lambda e, sl=sl, c=c, k=k: e.scalar_tensor_tensor(out=UCV[:, sl, 0:Wu], in0=UBUF[:, sl, k:k + Wu],
                                                                              scalar=par(P_SW + c * 3 + k), in1=UCV[:, sl, 0:Wu],
                                                                              op0=ALU.mult, op1=ALU.add))
                    for (seq, s0, L) in pts:
                        P.op("act", [("ubuf", sl), RG], [hres("U", seq)],
                             lambda e, sl=sl, s0=s0, seq=seq, L=L, c=c: e.activation(out=HU[:, seq, c, :], in_=UBUF[:, sl, s0 + L: s0 + L + 2], func=AF.Copy))
                    P.op("dve", [("uc", sl), RG], [("bank", pb), ("bin", c)],
                         lambda e, pb=pb, sl=sl, c=c: e.tensor_tensor(out=BIN[:, c, 0:T], in0=banks[pb][:, 0:T], in1=UCV[:, sl, 0:T], op=ALU.mult))
                    drain_bg(6)

            def emit_ln():
                def seg_copy_act(func, c, slot):
                    for si, (seq, L) in enumerate(segs):
                        P.op("act", [("aconv", c), RG], [("sq", slot)],
                             lambda e, si=si, L=L: e.activation(out=SQ[:, slot, co[si]:co[si] + L], in_=ACONV[:, c, ao[si]:ao[si] + L], func=func))
                b_mean = norm_stats(T, lambda k, slot: seg_copy_act(AF.Copy, k, slot), 1, CC)
                b_ex2 = norm_stats(T, lambda k, slot: seg_copy_act(AF.Square, k, slot), 1, CC, b=next_bank())
                P.op("dve", [], [("bank", b_mean), ("st", 1)], lambda e, T=T, b_mean=b_mean: e.tensor_copy(out=ST[:, 1, 0:T], in_=banks[b_mean][:, 0:T]))
                P.op("dve", [("st", 1)], [("st", 2)], lambda e, T=T: e.tensor_tensor(out=ST[:, 2, 0:T], in0=ST[:, 1, 0:T], in1=ST[:, 1, 0:T], op=ALU.mult))
                P.op("dve", [], [("bank", b_ex2), ("st", 2)], lambda e, T=T, b_ex2=b_ex2: e.tensor_tensor(out=ST[:, 2, 0:T], in0=banks[b_ex2][:, 0:T], in1=ST[:, 2, 0:T], op=ALU.subtract))
                P.op("act", ["EPS"], [("st", 2)], lambda e, T=T: e.activation(out=ST[:, 2, 0:T], in_=ST[:, 2, 0:T], func=AF.Sqrt, bias=EPSB, scale=1.0))
                P.op("dve", [], [("st", 2)], lambda e, T=T: e.reciprocal(out=ST[:, 2, 0:T], in_=ST[:, 2, 0:T]))
                P.op("dve", [("st", 2)], [("st", 1)], lambda e, T=T: e.tensor_tensor(out=ST[:, 1, 0:T], in0=ST[:, 1, 0:T], in1=ST[:, 2, 0:T], op=ALU.mult))
                for c0_ in range(0, CC, 2):
                    for c in (c0_, c0_ + 1):
                        P.op("dve", [("st", 2), RG], [("aconv", c)],
                             lambda e, c=c: e.tensor_tensor(out=ACONV[:, c, 0:T], in0=ACONV[:, c, 0:T], in1=ST[:, 2, 0:T], op=ALU.mult))
                    for c in (c0_, c0_ + 1):
                        P.op("dve", [("st", 1), RG], [("aconv", c)],
                             lambda e, c=c: e.tensor_tensor(out=ACONV[:, c, 0:T], in0=ACONV[:, c, 0:T], in1=ST[:, 1, 0:T], op=ALU.subtract))
                    for c in (c0_, c0_ + 1):
                        P.op("act", [("aconv", c), "PAR", RG], [("sa", c)],
                             lambda e, c=c: e.activation(out=SA[:, c, 0:T], in_=ACONV[:, c, 0:T],
                                                         func=AF.Silu, bias=par(P_LB + c), scale=par(P_LG + c)))

            if ti >= 1:
                emit_xcopy()

            for i in range(8):
                ba, bb = seq_gate[i]
                if i == 5:
                    drain_bg(len(bgq))
                    emit_ln()
                for cc in range(2):
                    m = 2 * i + cc
                    for which, blk in ((0, ba), (1, bb)):
                        pg = win_group(blk, cc, ba)
                        gi = which * 16 + m
                        P.op("act", [RG], [("bank", pg), ("gat", gi)],
                             lambda e, pg=pg, gi=gi, T=T: e.activation(out=GAT[:, gi, 0:T], in_=banks[pg][:, 0:T], func=AF.Sigmoid))
                        drain_bg(8)

            SAres = [("sa", c) for c in range(CC)]
            SArhs = [SA[:, c, 0:T] for c in range(CC)]
            BIres = [("bin", c) for c in range(CC)]
            BIrhs = [BIN[:, c, 0:T] for c in range(CC)]
            for q in range(4):
                b1, b2 = seq_out[q]
                for ml in range(4):
                    m = 4 * q + ml
                    sl = m % 2
                    pa = pe_group([gb0 + b1], [(gb0 + b1, k, ml) for k in range(CC)], SArhs, SAres, T, [RG], oldest=gb0 + b1)
                    pb = pe_group([gb0 + b2], [(gb0 + b2, k, ml) for k in range(CC)], BIrhs, BIres, T, [RG], oldest=gb0 + b1)
                    P.op("dve", [("gat", m), RG], [("bank", pa), ("t12", sl)],
                         lambda e, pa=pa, sl=sl, m=m, T=T: e.tensor_tensor(out=T12[:, sl, 0:T], in0=banks[pa][:, 0:T], in1=GAT[:, m, 0:T], op=ALU.mult))
                    P.op("dve", [("gat", 16 + m), RG], [("bank", pb), ("t12", 2 + sl)],
                         lambda e, pb=pb, sl=sl, m=m, T=T: e.tensor_tensor(out=T12[:, 2 + sl, 0:T], in0=banks[pb][:, 0:T], in1=GAT[:, 16 + m, 0:T], op=ALU.mult))
                    P.op("dve", [("t12", sl), ("t12", 2 + sl), RG], [("mg", m)],
                         lambda e, sl=sl, m=m, T=T: e.tensor_tensor(out=MG[:, m, 0:T], in0=T12[:, sl, 0:T], in1=T12[:, 2 + sl, 0:T], op=ALU.add))

            MGres = [("mg", k) for k in range(KC)]
            MGrhs = [MG[:, k, 0:T] for k in range(KC)]
            mix_banks = []

            def evac_to_M(pm, m, gcol, slot, T=T):
                P.op("act", [], [("bank", pm), ("sq", slot)],
                     lambda e: e.activation(out=SQ[:, slot, 0:T], in_=banks[pm][:, 0:T], func=AF.Square))
                P.op("act", ["PAR"], [("bank", pm), ("M", m)],
                     lambda e: e.activation(out=M[:, m, 0:T], in_=banks[pm][:, 0:T], func=AF.Identity, scale=par(gcol + m)))

            bstat = 7

            def stat_mm(m):
                P.op("pe", [("sq", m % 4), "ONES"], [("bank", bstat)],
                     lambda e: e.matmul(banks[bstat][:, 0:T], lhsT=ONES[:, 0, :], rhs=SQ[:, m % 4, 0:T],
                                        start=(m == 0), stop=(m == KC - 1)))
            for i in range(8):
                blk = seq_wo[i]
                for ml in range(2):
                    m = 2 * i + ml
                    pm = pe_group([gb0 + blk], [(gb0 + blk, k, ml) for k in range(KC)], MGrhs, MGres, T, [RG])
                    evac_to_M(pm, m, P_G2, m % 4)
                    if m >= 2:
                        stat_mm(m - 2)
            stat_mm(KC - 2)
            stat_mm(KC - 1)
            rstd_from_bank(bstat, T, 0)
            def x1_mul(xeng, m):
                P.op(xeng, [("st", 0)], [("M", m)],
                     lambda e, m=m, T=T: e.tensor_tensor(out=M[:, m, 0:T], in0=M[:, m, 0:T], in1=ST[:, 0, 0:T], op=ALU.mult))

            def x1_add(xeng, m):
                P.op(xeng, [("M", m)], [("X", m)],
                     lambda e, m=m, T=T: e.tensor_tensor(out=X[:, m, 0:T], in0=X[:, m, 0:T], in1=M[:, m, 0:T], op=ALU.add))
            n_dve = KC if ti <= 1 else 11
            x1_mul("dve", 0)
            for m in range(n_dve):
                if m + 1 < n_dve:
                    x1_mul("dve", m + 1)
                x1_add("dve", m)
            for m in range(n_dve, KC):
                x1_mul("pool", m)
                x1_add("pool", m)

            b = norm_stats(T, sq_x, 0, KC)
            rstd_from_bank(b, T, 3)
            for k in range(KC):
                P.op("dve", [("X", k), ("st", 3), "PAR"], [("H", k)],
                     lambda e, k=k, T=T: e.scalar_tensor_tensor(out=H[:, k, 0:T], in0=X[:, k, 0:T], scalar=par(P_G3 + k),
                                                               in1=ST[:, 3, 0:T], op0=ALU.mult, op1=ALU.mult))

            P.op("dve", [], [RG], lambda e: e.memset(DUMMY, 0.0))

            has_next = ti + 1 < len(TILES)
            if has_next:
                Tn = TW
                coln = col0 + T
                xsrcn = xT[:, coln:coln + Tn].rearrange("(kc p) t -> p kc t", p=128)
                P.op("sp", [], [("gat", j) for j in range(32)],
                     lambda e: e.dma_start(out=XN[:, :, 0:Tn], in_=xsrcn), sem=s_x)

            def pro_sq(k):
                P.op("act", [("gat", 2 * k), ("gat", 2 * k + 1)], [("sq", k % 4)],
                     lambda e: e.activation(out=SQ[:, k % 4, 0:Tn], in_=XN[:, k, 0:Tn], func=AF.Square))

            def pro_mm(k):
                P.op("pe", [("sq", k % 4), "ONES"], [("bank", 7)],
                     lambda e: e.matmul(banks[7][:, 0:Tn], lhsT=ONES[:, 0, :], rhs=SQ[:, k % 4, 0:Tn],
                                        start=(k == 0), stop=(k == KC - 1)))

            ubc = [0]
            for i in range(22):
                if has_next and 10 <= i < 18:
                    pro_sq(2 * (i - 10))
                    pro_sq(2 * (i - 10) + 1)
                if has_next and 11 <= i < 19:
                    pro_mm(2 * (i - 11))
                    pro_mm(2 * (i - 11) + 1)
                if has_next and i == 19:
                    rstd_from_bank(7, Tn, 1)
                b1, b2 = seq_up[i]
                for cc in range(2):
                    j = 2 * i + cc
                    slots = []
                    for which, blk in ((0, b1), (1, b2)):
                        cidx = j + which * FC
                        pu = win_group(blk, cc, b1)
                        sl = ubc[0] % 4
                        ubc[0] += 1
                        slots.append(sl)
                        P.op("act", [RG], [("bank", pu), ("ub", sl)],
                             lambda e, sl=sl, pu=pu: e.activation(out=UB[:, sl, 2:2 + T], in_=banks[pu][:, 0:T], func=AF.Copy))
                        for (seq, s0, L) in pts:
                            P.op("act", hist_reads("F", seq) + [RG], [("ub", sl)],
                                 lambda e, sl=sl, s0=s0, seq=seq, cidx=cidx: e.activation(out=UB[:, sl, s0:s0 + 2], in_=HF[:, seq, cidx, :], func=AF.Copy))
                            P.op("act", [], [("bank", pu), hres("F", seq)],
                                 lambda e, s0=s0, seq=seq, L=L, pu=pu, cidx=cidx: e.activation(out=HF[:, seq, cidx, :], in_=banks[pu][:, s0 + L - 2:s0 + L], func=AF.Copy))
                        P.op("dve", [("ub", sl), "PAR", RG], [("cv", sl)],
                             lambda e, sl=sl, cidx=cidx: e.tensor_scalar(out=CV[:, sl, 0:Wu], in0=UB[:, sl, 2:2 + Wu], scalar1=par(P_FW + cidx * 3 + 2),
                                                                         scalar2=None, op0=ALU.mult))
                        for k in (1, 0):
                            P.op("dve", [("ub", sl), "PAR", RG], [("cv", sl)],
                                 lambda e, sl=sl, cidx=cidx, k=k: e.scalar_tensor_tensor(out=CV[:, sl, 0:Wu], in0=UB[:, sl, k:k + Wu],
                                                                                        scalar=par(P_FW + cidx * 3 + k), in1=CV[:, sl, 0:Wu],
                                                                                        op0=ALU.mult, op1=ALU.add))
                    sg_, sv_ = slots
                    ss = j % 2
                    for si, (seq, L) in enumerate(segs):
                        P.op("act", [("cv", sg_), RG], [("sil", ss)],
                             lambda e, sg_=sg_, ss=ss, si=si, L=L: e.activation(out=SIL[:, ss, co[si]:co[si] + L], in_=CV[:, sg_, uo[si]:uo[si] + L], func=AF.Silu))
                        P.op("dve", [("sil", ss), ("cv", sv_), RG], [("g", j)],
                             lambda e, sv_=sv_, ss=ss, si=si, L=L, j=j: e.tensor_tensor(out=G[:, j, co[si]:co[si] + L], in0=SIL[:, ss, co[si]:co[si] + L],
                                                                                       in1=CV[:, sv_, uo[si]:uo[si] + L], op=ALU.mult))

            if has_next:
                for k in range(KC):
                    P.op("dve", [("gat", 2 * k), ("gat", 2 * k + 1), ("st", 1), "PAR"], [("H", k)],
                         lambda e, k=k: e.scalar_tensor_tensor(out=H[:, k, 0:Tn], in0=XN[:, k, 0:Tn], scalar=par(P_G1 + k),
                                                              in1=ST[:, 1, 0:Tn], op0=ALU.mult, op1=ALU.mult))

            Gres = [("g", j) for j in range(FC)]
            bstat = 7
            for m in range(16):
                b1, b2 = seq_dn[m]
                parts = [(gb0 + b1, k, 0) for k in range(22)] + [(gb0 + b2, k, 0) for k in range(22)]
                pf = pe_group([gb0 + b1, gb0 + b2], parts, [G[:, j, 0:T] for j in range(FC)], Gres, T, [RG])
                evac_to_M(pf, m, P_G4, m % 4)
                if m >= 1:
                    stat_mm(m - 1)
            stat_mm(KC - 1)
            rstd_from_bank(bstat, T, 0)
            yeng = "pool" if (has_next and ti >= 1) else "dve"
            for m in range(KC):
                P.op(yeng, [("st", 0)], [("M", m)],
                     lambda e, m=m, T=T: e.tensor_tensor(out=M[:, m, 0:T], in0=M[:, m, 0:T], in1=ST[:, 0, 0:T], op=ALU.mult))
                P.op(yeng, [("X", m)], [("M", m)],
                     lambda e, m=m, T=T: e.tensor_tensor(out=M[:, m, 0:T], in0=M[:, m, 0:T], in1=X[:, m, 0:T], op=ALU.add))
            ydst = yT[:, col0:col0 + T].rearrange("(kc p) t -> p kc t", p=128)
            P.op("pool", [("M", m) for m in range(KC)], ["Y"],
                 lambda e, ydst=ydst, T=T: e.dma_start(out=ydst, in_=M[:, :, 0:T]), sem=s_y)

        col0 = 0
        for ti, pts in enumerate(TILES):
            emit_tile(ti, pts, col0)
            col0 += TW

        def emit_outs(e):
            ins = []
            for s in range(3):
                ins.append(e.dma_start(out=oA[s], in_=HA[:, s].rearrange("p c t -> p (c t)")))
                ins.append(e.dma_start(out=oU[s], in_=HU[:, s].rearrange("p c t -> p (c t)")))
                ins.append(e.dma_start(out=oF[s], in_=HF[:, s].rearrange("p c t -> p (c t)")))
            return ins
        allh = [hres(k, s) for k in "AUF" for s in range(3)] + ["HA0", "HU0", "HF0", "HA12", "HU12", "HF12"]
        P.op("sp", allh, ["OUT"], emit_outs, sem=s_out, ninc=9)
        final_sems = [s_y, s_out, s_x, s_const] + s_wl + s_wb

        @block.tensor
        def _(e):
            P.run("pe", e)

        @block.scalar
        def _(e):
            P.run("act", e)

        @block.vector
        def _(e):
            P.run("dve", e)

        @block.gpsimd
        def _(e):
            P.run("pool", e)

        @block.sync
        def _(e):
            P.run("sp", e)
            for s in final_sems:
                if s.val > 0:
                    e.wait_ge(s.h, s.val)
            for k in ("pe", "act", "dve"):
                e.wait_ge(csem[k].h, csem[k].val)
    return nc


_NC_CACHE = {}


def _get_nc():
    if "nc" not in _NC_CACHE:
        _NC_CACHE["nc"] = build_nc()
    return _NC_CACHE["nc"]


def _pack_params(g1, g2, g3, g4, cb, lg, lb, cw, sw, fw):
    p = np.zeros((128, NPAR), np.float32)
    p[:, P_G1:P_G1 + 16] = g1.reshape(16, 128).T
    p[:, P_G2:P_G2 + 16] = g2.reshape(16, 128).T
    p[:, P_G3:P_G3 + 16] = g3.reshape(16, 128).T
    p[:, P_G4:P_G4 + 16] = g4.reshape(16, 128).T
    p[:, P_CB:P_CB + 8] = cb.reshape(8, 128).T
    p[:, P_LG:P_LG + 8] = lg.reshape(8, 128).T
    p[:, P_LB:P_LB + 8] = lb.reshape(8, 128).T
    p[:, P_CW:P_CW + 248] = cw.T.reshape(8, 128, 31).transpose(1, 0, 2).reshape(128, 248)
    p[:, P_SW:P_SW + 24] = sw.T.reshape(8, 128, 3).transpose(1, 0, 2).reshape(128, 24)
    p[:, P_FW:P_FW + 264] = fw.T.reshape(88, 128, 3).transpose(1, 0, 2).reshape(128, 264)
    return p


def _state_to_dev(st):
    w1, C = st.shape
    return np.ascontiguousarray(st.T.reshape(C // 128, 128, w1).transpose(1, 0, 2).reshape(128, -1))


def _state_from_dev(a, w1):
    ncn = a.shape[1] // w1
    return np.ascontiguousarray(a.reshape(128, ncn, w1).transpose(2, 1, 0).reshape(w1, ncn * 128))


def kernel(x_prompt, x_sample, state_conf_conv, state_sconv, state_ffn_conv, meta_tokens,
           g_pre_mix, w_in, conf_conv_w, conf_conv_b, conf_ln_g, conf_ln_b, w_conf_out,
           sconv_w, w_sconv_out, w_o, g_post_mix, g_pre_ffn, w_up, ffn_conv_w, w_down, g_post_ffn):
    f = lambda a: np.asarray(a, dtype=np.float32)
    x_prompt, x_sample, meta_tokens = f(x_prompt), f(x_sample), f(meta_tokens)
    sA, sU, sF = f(state_conf_conv)[0], f(state_sconv)[0], f(state_ffn_conv)[0]
    params = _pack_params(f(g_pre_mix)[0], f(g_post_mix)[0], f(g_pre_ffn)[0], f(g_post_ffn)[0], f(conf_conv_b)[0],
                          f(conf_ln_g)[0], f(conf_ln_b)[0], f(conf_conv_w)[0], f(sconv_w)[0], f(ffn_conv_w)[0])
    wsrc = {"w_in": f(w_in)[0], "w_conf_out": f(w_conf_out)[0], "w_sconv_out": f(w_sconv_out)[0],
            "w_o": f(w_o)[0], "w_up": f(w_up)[0], "w_down": f(w_down)[0]}
    blocks = make_blocks()[0]
    wblk = np.zeros((len(blocks), 128, SLOT_ELEMS), np.float32)
    for bi, (wn, r0, nk, c0, ncols) in enumerate(blocks):
        blk = wsrc[wn][r0:r0 + nk * 128, c0:c0 + ncols]
        wblk[bi, :, :nk * ncols] = blk.reshape(nk, 128, ncols).transpose(1, 0, 2).reshape(128, nk * ncols)
    shared = {"params": params, "wblk": wblk}
    in_maps = []
    for c in range(NCORES):
        b, half = c // 2, c % 2
        ext = np.concatenate([meta_tokens, x_prompt[b]], axis=0)
        part = ext[0:HALF] if half == 0 else ext[EXT - HALF:EXT]
        zg = np.zeros((GAP, D), np.float32)
        rows = np.concatenate([part, zg, x_sample[2 * c], zg, x_sample[2 * c + 1]], axis=0)
        m = dict(shared)
        m["xT"] = np.ascontiguousarray(rows.T)
        m["stA"] = np.stack([_state_to_dev(sA[2 * c + i]) for i in range(2)])
        m["stU"] = np.stack([_state_to_dev(sU[2 * c + i]) for i in range(2)])
        m["stF"] = np.stack([_state_to_dev(sF[2 * c + i]) for i in range(2)])
        in_maps.append(m)
    nc = _get_nc()
    res = run_bass_kernel_spmd(nc, in_maps, core_ids=list(range(NCORES)))
    B = x_prompt.shape[0]
    y_prompt = np.zeros((B, SEQ, D), np.float32)
    y_sample = np.zeros((2 * NCORES, SL, D), np.float32)
    ncp = np.zeros((1, B, 30, DC), np.float32)
    nsp = np.zeros((1, B, 2, DC), np.float32)
    nfp = np.zeros((1, B, 2, 2 * DFF), np.float32)
    ncs = np.zeros((1, 2 * NCORES, 30, DC), np.float32)
    nss = np.zeros((1, 2 * NCORES, 2, DC), np.float32)
    nfs = np.zeros((1, 2 * NCORES, 2, 2 * DFF), np.float32)
    first = HALF - NMETA
    for c in range(NCORES):
        r = res.results[c]
        b, half = c // 2, c % 2
        y = np.asarray(r["yT"]).T
        if half == 0:
            y_prompt[b, 0:first] = y[NMETA:HALF]
        else:
            y_prompt[b, first:SEQ] = y[HALO:HALF]
            ncp[0, b] = _state_from_dev(np.asarray(r["oA"])[0], 30)
            nsp[0, b] = _state_from_dev(np.asarray(r["oU"])[0], 2)
            nfp[0, b] = _state_from_dev(np.asarray(r["oF"])[0], 2)
        for i in range(2):
            y_sample[2 * c + i] = y[HALF + GAP + (GAP + SL) * i: HALF + (GAP + SL) * (i + 1)]
            ncs[0, 2 * c + i] = _state_from_dev(np.asarray(r["oA"])[1 + i], 30)
            nss[0, 2 * c + i] = _state_from_dev(np.asarray(r["oU"])[1 + i], 2)
            nfs[0, 2 * c + i] = _state_from_dev(np.asarray(r["oF"])[1 + i], 2)
    return (y_prompt, y_sample, ncp, nsp, nfp, ncs, nss, nfs)
```
